# Optimizing a Trainium2 kernel written in Bass

```python
import math
import jax, jax.numpy as jnp
from jax import lax
import numpy as np

D_MODEL = 1024
BATCH = 16
SEQ = 256
DEPTH = 2
DEC_BATCH = 8
DEC_SEQ = 1024
PAST_LEN = 512

GRID_W = 64
N_MIXERS = 2
N_HY_LAYERS = (DEPTH + 1) // 2
N_S5_LAYERS = DEPTH // 2
D_FF = -(-8 * D_MODEL // (3 * 256)) * 256
EPS = 1e-6
POS_BASE = 10000.0
HY_ORDER = 2
HY_EMB = 33
HY_BANDS = (HY_EMB - 1) // 2
HY_FO = 64
HY_SHORT = 3
HY_TARGET = 1e-2
HY_FAST = 0.3
HY_SLOW = 1.5
S5_GROUP = 16
S5_G = D_MODEL // S5_GROUP
S5_P = 64

kernel_name = 'hyena_s5_prefix_diffusion_step'


def _rmsnorm(x, g):
    xf = x.astype(jnp.float32)
    ms = jnp.mean(xf * xf, axis=-1, keepdims=True)
    return (xf * lax.rsqrt(ms + EPS) * g.astype(jnp.float32)).astype(x.dtype)


def _modulate(h, shift, scale):
    return h * (1 + scale[:, None]) + shift[:, None]


def _swiglu(h, w13, w2):
    a, b = jnp.split(h @ w13, 2, axis=-1)
    return (jax.nn.silu(a) * b) @ w2


def _grid_pos_embed(rows):
    quarter = D_MODEL // 4
    omega = 1.0 / (POS_BASE ** (jnp.arange(quarter, dtype=jnp.float32) / quarter))

    def axis_embed(n):
        ang = jnp.arange(n, dtype=jnp.float32)[:, None] * omega[None]
        return jnp.concatenate([jnp.sin(ang), jnp.cos(ang)], axis=-1)

    er = jnp.broadcast_to(axis_embed(rows)[:, None], (rows, GRID_W, D_MODEL // 2))
    ec = jnp.broadcast_to(axis_embed(GRID_W)[None], (rows, GRID_W, D_MODEL // 2))
    return jnp.concatenate([er, ec], axis=-1).reshape(rows * GRID_W, D_MODEL)


def _short_conv(x, w, b):
    L = x.shape[1]
    xp = jnp.pad(x, ((0, 0), (1, 1), (0, 0)))
    return xp[:, :L] * w[0] + xp[:, 1:L + 1] * w[1] + xp[:, 2:] * w[2] + b


def _hyena_filter_spectrum(L, w1, b1, w2, b2, w3, freq):
    f32 = jnp.float32
    t = jnp.linspace(0.0, 1.0, L, dtype=f32)[:, None]
    w = 2.0 * math.pi * jnp.arange(L, dtype=f32)[:, None] / L
    bands = jnp.linspace(1e-4, HY_BANDS - 1, HY_BANDS, dtype=f32)[None, :]
    z = jnp.concatenate([t, jnp.cos(bands * w), -jnp.sin(bands * w)], axis=-1)
    fr = freq.astype(f32)
    h = jnp.sin(fr * (z @ w1.astype(f32) + b1.astype(f32)))
    h = jnp.sin(fr * (h @ w2.astype(f32) + b2.astype(f32)))
    h = (h @ w3.astype(f32)).reshape(L, HY_ORDER, 2, D_MODEL)
    max_decay = math.log(HY_TARGET) / HY_FAST
    min_decay = math.log(HY_TARGET) / HY_SLOW
    deltas = jnp.abs(jnp.linspace(min_decay, max_decay, D_MODEL, dtype=f32))
    decay = jnp.exp(-t * deltas[None, :])
    h = h * decay[:, None, None, :]
    h = h / (jnp.sum(jnp.abs(h), axis=(0, 2), keepdims=True) + EPS)
    fwd = h[:, :, 0]
    bwd = h[:, :, 1]
    k = jnp.concatenate([fwd, jnp.zeros((1, HY_ORDER, D_MODEL), f32), bwd[:0:-1]], axis=0)
    return jnp.fft.rfft(k, axis=0)


def _fft_longconv(v, k_f, bias):
    L = v.shape[1]
    vf = v.astype(jnp.float32)
    v_f = jnp.fft.rfft(vf, n=2 * L, axis=1)
    y = jnp.fft.irfft(v_f * k_f[None], n=2 * L, axis=1)[:, :L]
    return y + vf * bias.astype(jnp.float32)


def _hyena_mixer(u, j, p):
    L = u.shape[1]
    z = u @ p['hy_in_w'][j] + p['hy_in_b'][j]
    z = _short_conv(z, p['hy_conv_w'][j], p['hy_conv_b'][j])
    v, x1, x2 = jnp.split(z, 3, axis=-1)
    k_f = _hyena_filter_spectrum(L, p['hy_pe_w1'][j], p['hy_pe_b1'][j], p['hy_pe_w2'][j],
                                 p['hy_pe_b2'][j], p['hy_pe_w3'][j], p['hy_freq'][j])
    fb = p['hy_fbias'][j]
    v = _fft_longconv(v, k_f[:, 0], fb[0]) * x1.astype(jnp.float32)
    v = _fft_longconv(v, k_f[:, 1], fb[1]) * x2.astype(jnp.float32)
    return (v @ p['hy_out_w'][j].astype(jnp.float32) + p['hy_out_b'][j].astype(jnp.float32)).astype(u.dtype)


def _s5_discretize(a_re, a_im, log_dt, b_re, b_im):
    f32 = jnp.float32
    a_re = a_re.astype(f32)
    a_im = a_im.astype(f32)
    b_re = b_re.astype(f32)
    b_im = b_im.astype(f32)
    dt = jnp.exp(log_dt.astype(f32))[:, None]
    mag = jnp.exp(dt * a_re)
    ph = dt * a_im
    ab_re = mag * jnp.cos(ph)
    ab_im = mag * jnp.sin(ph)
    nr = ab_re - 1.0
    ni = ab_im
    den = a_re * a_re + a_im * a_im
    co_re = ((nr * a_re + ni * a_im) / den)[..., None]
    co_im = ((ni * a_re - nr * a_im) / den)[..., None]
    bb_re = co_re * b_re - co_im * b_im
    bb_im = co_re * b_im + co_im * b_re
    return ab_re, ab_im, bb_re, bb_im


def _cmul_combine(e1, e2):
    a1r, a1i, b1r, b1i = e1
    a2r, a2i, b2r, b2i = e2
    return (a2r * a1r - a2i * a1i, a2r * a1i + a2i * a1r,
            a2r * b1r - a2i * b1i + b2r, a2r * b1i + a2i * b1r + b2i)


def _s5_scan(ug, ab_re, ab_im, bb_re, bb_im, init):
    L = ug.shape[1]
    bu_re = jnp.einsum('blgh,gph->blgp', ug, bb_re)
    bu_im = jnp.einsum('blgh,gph->blgp', ug, bb_im)
    a_re = jnp.broadcast_to(ab_re[None, None], (1, L) + ab_re.shape)
    a_im = jnp.broadcast_to(ab_im[None, None], (1, L) + ab_im.shape)
    pr, pi, hr, hi = lax.associative_scan(_cmul_combine, (a_re, a_im, bu_re, bu_im), axis=1)
    if init is not None:
        h0r = init[0][:, None]
        h0i = init[1][:, None]
        hr = hr + pr * h0r - pi * h0i
        hi = hi + pr * h0i + pi * h0r
    return hr, hi


def _s5_mixer(u, j, p, h0):
    f32 = jnp.float32
    bsz, L, _ = u.shape
    uf = u.astype(f32)
    ug = uf.reshape(bsz, L, S5_G, S5_GROUP)
    y = uf * p['s5_D'][j].astype(f32)
    finals = []
    for d in range(2):
        ab_re, ab_im, bb_re, bb_im = _s5_discretize(p['s5_A_re'][j, d], p['s5_A_im'][j, d], p['s5_log_dt'][j, d],
                                                    p['s5_B_re'][j, d], p['s5_B_im'][j, d])
        src = ug if d == 0 else jnp.flip(ug, axis=1)
        init = None if h0 is None else (h0[:, d, 0].astype(f32), h0[:, d, 1].astype(f32))
        hr, hi = _s5_scan(src, ab_re, ab_im, bb_re, bb_im, init)
        yd = (jnp.einsum('ghp,blgp->blgh', p['s5_C_re'][j, d].astype(f32), hr)
              - jnp.einsum('ghp,blgp->blgh', p['s5_C_im'][j, d].astype(f32), hi))
        if d == 1:
            yd = jnp.flip(yd, axis=1)
        y = y + yd.reshape(bsz, L, D_MODEL)
        if h0 is None:
            finals.append(jnp.stack([hr[:, -1], hi[:, -1]], axis=1))
    y = jax.nn.gelu(y)
    a, g = jnp.split(y @ p['s5_glu_w'][j].astype(f32) + p['s5_glu_b'][j].astype(f32), 2, axis=-1)
    out = (a * jax.nn.sigmoid(g)).astype(u.dtype)
    fin = jnp.stack(finals, axis=1) if h0 is None else None
    return out, fin


def _trunk(x, cond, s5_state, p):
    finals = []
    for i in range(DEPTH):
        j = i // N_MIXERS
        mod = jax.nn.silu(cond) @ p['ada_w'][i] + p['ada_b'][i]
        sh1, sc1, g1, sh2, sc2, g2 = jnp.split(mod.astype(x.dtype), 6, axis=-1)
        h = _modulate(_rmsnorm(x, p['norm1_g'][i]), sh1, sc1)
        if i % N_MIXERS == 0:
            m = _hyena_mixer(h, j, p)
        else:
            h0 = None if s5_state is None else s5_state[:, j]
            m, fin = _s5_mixer(h, j, p, h0)
            if s5_state is None:
                finals.append(fin)
        x = x + g1[:, None] * m
        h = _modulate(_rmsnorm(x, p['norm2_g'][i]), sh2, sc2)
        x = x + g2[:, None] * _swiglu(h, p['ffn_w13'][i], p['ffn_w2'][i])
    y = _rmsnorm(x, p['final_g'])
    new_state = jnp.stack(finals, axis=1) if s5_state is None else None
    return y, new_state


def setup_inputs(seed: int = 0) -> dict:
    key = jax.random.key(seed)
    ks = list(jax.random.split(key, 40))
    f32 = jnp.float32

    def nrm(k, shape, s):
        return jax.random.normal(k, shape, f32) * s

    D, F, NH, NS = D_MODEL, D_FF, N_HY_LAYERS, N_S5_LAYERS
    G, P, H = S5_G, S5_P, S5_GROUP
    n_idx = jnp.arange(P, dtype=f32)
    return {
        'x_prompt': nrm(ks[0], (BATCH, SEQ, D), 1.0),
        'x_sample': nrm(ks[1], (DEC_BATCH, DEC_SEQ, D), 1.0),
        'state_s5': nrm(ks[2], (DEC_BATCH, NS, 2, 2, G, P), 0.1),
        'c': nrm(ks[3], (DEC_BATCH, D), 1.0),
        'c_ctx': nrm(ks[4], (D,), 1.0),
        'norm1_g': 1.0 + nrm(ks[5], (DEPTH, D), 0.01),
        'norm2_g': 1.0 + nrm(ks[6], (DEPTH, D), 0.01),
        'final_g': 1.0 + nrm(ks[7], (D,), 0.01),
        'ada_w': nrm(ks[8], (DEPTH, D, 6 * D), 0.5 * D ** -0.5),
        'ada_b': nrm(ks[9], (DEPTH, 6 * D), 0.01),
        'ffn_w13': nrm(ks[10], (DEPTH, D, 2 * F), D ** -0.5),
        'ffn_w2': nrm(ks[11], (DEPTH, F, D), F ** -0.5),
        'hy_in_w': nrm(ks[12], (NH, D, 3 * D), D ** -0.5),
        'hy_in_b': nrm(ks[13], (NH, 3 * D), 0.01),
        'hy_conv_w': nrm(ks[14], (NH, HY_SHORT, 3 * D), HY_SHORT ** -0.5),
        'hy_conv_b': nrm(ks[15], (NH, 3 * D), 0.01),
        'hy_pe_w1': nrm(ks[16], (NH, HY_EMB, HY_FO), HY_EMB ** -0.5),
        'hy_pe_b1': nrm(ks[17], (NH, HY_FO), 0.1),
        'hy_pe_w2': nrm(ks[18], (NH, HY_FO, HY_FO), HY_FO ** -0.5),
        'hy_pe_b2': nrm(ks[19], (NH, HY_FO), 0.1),
        'hy_pe_w3': nrm(ks[20], (NH, HY_FO, HY_ORDER * 2 * D), HY_FO ** -0.5),
        'hy_freq': 1.0 + nrm(ks[21], (NH, HY_FO), 0.1),
        'hy_fbias': nrm(ks[22], (NH, HY_ORDER, D), 0.5),
        'hy_out_w': nrm(ks[23], (NH, D, D), D ** -0.5),
        'hy_out_b': nrm(ks[24], (NH, D), 0.01),
        's5_A_re': -0.5 + nrm(ks[25], (NS, 2, G, P), 0.01),
        's5_A_im': math.pi * n_idx + nrm(ks[26], (NS, 2, G, P), 0.01),
        's5_log_dt': jax.random.uniform(ks[27], (NS, 2, G), f32, math.log(1e-3), math.log(1e-1)),
        's5_B_re': nrm(ks[28], (NS, 2, G, P, H), (2 * H) ** -0.5),
        's5_B_im': nrm(ks[29], (NS, 2, G, P, H), (2 * H) ** -0.5),
        's5_C_re': nrm(ks[30], (NS, 2, G, H, P), P ** -0.5),
        's5_C_im': nrm(ks[31], (NS, 2, G, H, P), P ** -0.5),
        's5_D': nrm(ks[32], (NS, D), 1.0),
        's5_glu_w': nrm(ks[33], (NS, D, 2 * D), D ** -0.5),
        's5_glu_b': nrm(ks[34], (NS, 2 * D), 0.01),
    }


def reference(x_prompt, x_sample, state_s5, c, c_ctx, norm1_g, norm2_g, final_g, ada_w, ada_b,
              ffn_w13, ffn_w2, hy_in_w, hy_in_b, hy_conv_w, hy_conv_b, hy_pe_w1, hy_pe_b1,
              hy_pe_w2, hy_pe_b2, hy_pe_w3, hy_freq, hy_fbias, hy_out_w, hy_out_b,
              s5_A_re, s5_A_im, s5_log_dt, s5_B_re, s5_B_im, s5_C_re, s5_C_im, s5_D,
              s5_glu_w, s5_glu_b):
    p = dict(norm1_g=norm1_g, norm2_g=norm2_g, final_g=final_g, ada_w=ada_w, ada_b=ada_b,
             ffn_w13=ffn_w13, ffn_w2=ffn_w2, hy_in_w=hy_in_w, hy_in_b=hy_in_b,
             hy_conv_w=hy_conv_w, hy_conv_b=hy_conv_b, hy_pe_w1=hy_pe_w1, hy_pe_b1=hy_pe_b1,
             hy_pe_w2=hy_pe_w2, hy_pe_b2=hy_pe_b2, hy_pe_w3=hy_pe_w3, hy_freq=hy_freq,
             hy_fbias=hy_fbias, hy_out_w=hy_out_w, hy_out_b=hy_out_b, s5_A_re=s5_A_re,
             s5_A_im=s5_A_im, s5_log_dt=s5_log_dt, s5_B_re=s5_B_re, s5_B_im=s5_B_im,
             s5_C_re=s5_C_re, s5_C_im=s5_C_im, s5_D=s5_D, s5_glu_w=s5_glu_w, s5_glu_b=s5_glu_b)
    y_prompt, new_state_s5 = _trunk(x_prompt, c_ctx[None], None, p)
    rows = x_sample.shape[1] // GRID_W
    lat = x_sample + _grid_pos_embed(rows).astype(x_sample.dtype)[None]
    y_sample, _ = _trunk(lat, c, state_s5, p)
    return (y_prompt, y_sample, new_state_s5)
```

```python
import math
import os
import contextlib
import numpy as np
import ml_dtypes
import concourse.bass as bass
import concourse.mybir as mybir
from concourse.bass_utils import run_bass_kernel_spmd
from concourse.ap import AP

F32 = mybir.dt.float32
BF16 = mybir.dt.bfloat16
I32 = mybir.dt.int32
AF = mybir.ActivationFunctionType
ALU = mybir.AluOpType

D = 1024
DFF = 2816
NKF = 22
EPS = 1e-6
TWO_PI = 2.0 * math.pi


class Prog:
    def __init__(self, nc):
        self.nc = nc
        self.ops = []
        self.last_w = {}
        self.readers = {}
        self.dreaders = {}
        self.phase_tok = "__phase__"
        self.engs = {"pe": nc.tensor, "dve": nc.vector, "act": nc.scalar,
                     "pool": nc.gpsimd, "sp": nc.sync}

    def add(self, eng, fn, reads=(), writes=(), dma=None, small=False):
        idx = len(self.ops)
        deps = set()
        reads = tuple(reads) + (self.phase_tok,)
        for t in reads:
            w = self.last_w.get(t)
            if w is not None:
                deps.add(w)
        for t in writes:
            w = self.last_w.get(t)
            if w is not None:
                deps.add(w)
            r = self.readers.get(t)
            if r:
                deps.update(r.values())
            r = self.dreaders.get(t)
            if r:
                deps.update(r)
        need = set()
        for d in deps:
            o = self.ops[d]
            if o[3] is not None or o[0] != eng or (o[6] and eng != "pe"):
                need.add(d)
        self.ops.append([eng, fn, need, dma, False, 0, small])
        for t in reads:
            if dma is not None:
                self.dreaders.setdefault(t, []).append(idx)
            else:
                self.readers.setdefault(t, {})[eng] = idx
        for t in writes:
            self.last_w[t] = idx
            self.readers[t] = {}
            self.dreaders[t] = []
        return idx

    def emit(self, sems):
        ops = self.ops
        for o in ops:
            for d in o[2]:
                ops[d][4] = True
        cnt = {}
        for o in ops:
            if o[3] is not None:
                k = o[3]
                cnt[k] = cnt.get(k, 0) + 16
                o[5] = cnt[k]
            elif o[4]:
                k = o[0]
                cnt[k] = cnt.get(k, 0) + 1
                o[5] = cnt[k]
        seen = {e: {} for e in self.engs}
        for o in ops:
            eng, fn, need, dma, sig, val = o[:6]
            E = self.engs[eng]
            waits = {}
            for d in need:
                od = ops[d]
                k = od[3] if od[3] is not None else od[0]
                if od[5] > waits.get(k, 0):
                    waits[k] = od[5]
            for k, v in waits.items():
                if seen[eng].get(k, 0) < v:
                    E.wait_ge(sems[k], v)
                    seen[eng][k] = v
            ins = fn(E)
            if dma is not None:
                ins.then_inc(sems[dma], 16)
            elif sig:
                ins.then_inc(sems[eng], 1)
        return cnt


def _dma_keys():
    keys = ["init%d" % i for i in range(24)]
    keys += ["x_in", "pos", "ada0", "ada1", "ada2", "hyin0", "hyin1", "hyin2", "hyout0", "hyout1",
             "w13_0", "w13_1", "w13_2", "w2_0", "w2_1", "glu0", "glu1", "decf", "decb", "w3cb",
             "s5b", "s5c", "s5st", "yout", "nsout", "dbg", "pos0", "pos1", "s5t0", "s5t1", "s5A", "scrw0", "scrw1",
             "w13s0h0", "w13s0h1", "w13s1h0", "w13s1h1", "w2s0", "w2s1", "glus0h0", "glus0h1", "glus1h0", "glus1h1",
             "cw1_0", "cw1_1", "cw2_0", "cw2_1", "cw4_0", "cw4_1", "s5A1", "s5st1", "s5b1", "s5c1", "nsout1"]
    return keys


def _lay_pk(a):
    K = a.shape[0] // 128
    return np.ascontiguousarray(a.reshape(K, 128, *a.shape[1:]).swapaxes(0, 1))


def _dft_consts(L):
    t = np.arange(L, dtype=np.float64)[:, None]
    f = np.arange(L, dtype=np.float64)[None, :]
    th = np.pi * (f + 0.5) * (t + 0.5) / L
    C4 = np.cos(th).astype(np.float32)
    S4 = (-np.sin(th)).astype(np.float32)
    phi = np.pi * (np.arange(L, dtype=np.float64) + 0.5) / (2 * L)
    cph = _lay_pk(np.cos(phi).astype(np.float32)[:, None])[:, :, 0]
    sph = _lay_pk(np.sin(phi).astype(np.float32)[:, None])[:, :, 0]
    return _lay_pk(C4), _lay_pk(S4), np.ascontiguousarray(np.concatenate([cph, sph, -sph], axis=1))


def _hyena_consts(L):
    f32 = np.float32
    t = np.linspace(0.0, 1.0, L, dtype=f32)[:, None]
    w = (2.0 * math.pi * np.arange(L, dtype=f32)[:, None] / L).astype(f32)
    bands = np.linspace(1e-4, 15, 16, dtype=f32)[None, :]
    z = np.concatenate([t, np.cos(bands * w), -np.sin(bands * w)], axis=-1).astype(f32)
    zT = np.zeros((33, L + 1), f32)
    zT[:, :L] = z.T
    max_decay = math.log(1e-2) / 0.3
    min_decay = math.log(1e-2) / 1.5
    deltas = np.abs(np.linspace(min_decay, max_decay, D, dtype=f32))
    decay = np.exp(-t * deltas[None, :]).astype(f32)
    decb = np.zeros_like(decay)
    decb[:L - 1] = decay[1:]
    return zT, _lay_pk(decay), _lay_pk(decb)


def _pos_embed_T():
    f32 = np.float32
    rows, GW = 16, 64
    quarter = D // 4
    omega = (1.0 / (10000.0 ** (np.arange(quarter, dtype=f32) / quarter))).astype(f32)

    def axis_embed(n):
        ang = np.arange(n, dtype=f32)[:, None] * omega[None]
        return np.concatenate([np.sin(ang), np.cos(ang)], axis=-1).astype(f32)

    er = np.broadcast_to(axis_embed(rows)[:, None], (rows, GW, D // 2))
    ec = np.broadcast_to(axis_embed(GW)[None], (rows, GW, D // 2))
    pe = np.concatenate([er, ec], axis=-1).reshape(rows * GW, D)
    return np.ascontiguousarray(pe.T)


SP_OFF = {}


def _sp_layout():
    off = 0
    for name, n in [("n1g", 16), ("n2g", 16), ("fing", 8), ("adab", 96), ("hyinb", 24), ("convw", 72),
                    ("convb", 24), ("fbias", 16), ("hyoutb", 8), ("s5D", 8), ("glub", 16)]:
        SP_OFF[name] = off
        off += n
    return off


NSP = _sp_layout()


class Builder:
    def __init__(self, stop_after=None):
        self.stop_after = stop_after
        self.nc = bass.Bass("TRN2", target_bir_lowering=False)
        self.P = Prog(self.nc)
        self.st = contextlib.ExitStack()
        self.ninit = 0
        self.ps_rr = 0
        self.marks = []

    def din(self, name, shape, dt=F32):
        return self.nc.dram_tensor(name, list(shape), dt, kind="ExternalInput").ap()

    def dout(self, name, shape, dt=F32):
        return self.nc.dram_tensor(name, list(shape), dt, kind="ExternalOutput").ap()

    def sb(self, name, shape, dt=F32):
        return self.st.enter_context(self.nc.sbuf_tensor("sb_" + name, list(shape), dt))

    def mm(self, out, lhsT, rhs, start, stop, reads, writes):
        self.P.add("pe", lambda E: E.matmul(out, lhsT=lhsT, rhs=rhs, start=start, stop=stop), reads, writes)

    @staticmethod
    def _small(ap):
        n = 1
        for d in ap.shape[1:]:
            n *= d
        return n < 400

    def act(self, out, in_, func, reads, writes, bias=None, scale=None):
        kw = {}
        if bias is not None:
            kw["bias"] = bias
        if scale is not None:
            kw["scale"] = scale
        self.P.add("act", lambda E: E.activation(out=out, in_=in_, func=func, **kw), reads, writes, small=self._small(out))

    def tt(self, eng, out, in0, in1, op, reads, writes):
        self.P.add(eng, lambda E: E.tensor_tensor(out=out, in0=in0, in1=in1, op=op), reads, writes, small=self._small(out))

    def ts(self, eng, out, in0, s1, s2, op0, op1, reads, writes):
        if s2 is None:
            self.P.add(eng, lambda E: E.tensor_single_scalar(out=out, in_=in0, scalar=s1, op=op0), reads, writes,
                       small=self._small(out))
        else:
            self.P.add(eng, lambda E: E.tensor_scalar(out=out, in0=in0, scalar1=s1, scalar2=s2, op0=op0, op1=op1),
                       reads, writes, small=self._small(out))

    def stt(self, eng, out, in0, scalar, in1, op0, op1, reads, writes):
        self.P.add(eng, lambda E: E.scalar_tensor_tensor(out=out, in0=in0, scalar=scalar, in1=in1, op0=op0, op1=op1),
                   reads, writes, small=self._small(out))

    def cp(self, eng, out, in_, reads, writes):
        if eng == "act":
            self.P.add(eng, lambda E: E.copy(out=out, in_=in_), reads, writes, small=self._small(out))
        else:
            self.P.add(eng, lambda E: E.tensor_copy(out=out, in_=in_), reads, writes, small=self._small(out))

    def memset(self, eng, ap, val, writes):
        self.P.add(eng, lambda E: E.memset(ap, val), (), writes, small=self._small(ap))

    def recip(self, out, in_, reads, writes):
        self.P.add("dve", lambda E: E.reciprocal(out=out, in_=in_), reads, writes, small=self._small(out))

    def dma(self, q, out, in_, key, reads, writes):
        self.P.add(q, lambda E: E.dma_start(out=out, in_=in_), reads, writes, dma=key)

    def init_load(self, out, in_, token, cast=False):
        key = "init%d" % self.ninit
        self.ninit += 1
        self.dma("pool" if cast else "sp", out, in_, key, (), [token])

    def barrier(self):
        d = self.dummy
        self.P.add("dve", lambda E: E.memset(d[:, 0:1], 0.0), (), [self.P.phase_tok])

    def bank(self, lst):
        b = lst[self.ps_rr % len(lst)]
        self.ps_rr += 1
        return b

    def build(self):
        nc = self.nc
        with self.st:
            self._declare()
            self._init_loads()
            self.s5_prep()
            if self.stop_after is None or not self.stop_after.startswith("B"):
                self.run_pass("A")
            if self.stop_after is None or not self.stop_after.startswith("A"):
                self.run_pass("B")
            sems = {}
            for k in ["pe", "dve", "act", "pool", "sp"] + _dma_keys():
                sems[k] = self.st.enter_context(nc.semaphore(k))
            cnt = self.P.emit(sems)
            for k in ["yout", "nsout", "nsout1", "dbg"]:
                if k in cnt:
                    nc.sync.wait_ge(sems[k], cnt[k])
        return nc

    def _declare(self):
        s = self
        s.d_xA = s.din("xT_A", [D, 512])
        s.d_xB = s.din("xT_B", [D, 1024])
        s.d_pos = s.din("posT", [D, 1024])
        s.d_cond = s.din("condT", [128, 8, 2])
        s.d_smallp = s.din("smallp", [128, NSP])
        s.d_ident = s.din("ident", [128, 128])
        s.d_ada_w = s.din("ada_w", [2, D, 6 * D])
        s.d_w13 = s.din("ffn_w13", [2, D, 2 * DFF])
        s.d_w2 = s.din("ffn_w2", [2, DFF, D])
        s.d_hyin = s.din("hy_in_w", [D, 3 * D])
        s.d_hyout = s.din("hy_out_w", [D, D])
        s.d_glu = s.din("s5_glu_w", [D, 2 * D])
        s.d_pe_w1 = s.din("pe_w1", [33, 64])
        s.d_pe_w2 = s.din("pe_w2", [64, 64])
        s.d_pe_w3 = s.din("pe_w3", [64, 4096])
        s.d_pe_small = s.din("pe_small", [64, 3])
        s.d_c4 = {"A": s.din("c4_A", [128, 2, 256], BF16), "B": s.din("c4_B", [128, 8, 1024], BF16)}
        s.d_s4 = {"A": s.din("s4_A", [128, 2, 256], BF16), "B": s.din("s4_B", [128, 8, 1024], BF16)}
        s.d_phi = {"A": s.din("phi_A", [128, 6]), "B": s.din("phi_B", [128, 24])}
        s.d_zT = {"A": s.din("zT_A", [33, 257]), "B": s.din("zT_B", [33, 1025])}
        s.d_decf = {"A": s.din("decf_A", [128, 2, D]), "B": s.din("decf_B", [128, 8, D])}
        s.d_decb = {"A": s.din("decb_A", [128, 2, D]), "B": s.din("decb_B", [128, 8, D])}
        s.d_s5a = s.din("s5_a", [64, 3, 128])
        s.d_s5e = s.din("s5_e", [64, 32])
        s.d_s5B = s.din("s5_Bt", [64, 2, 2, 64, 16])
        s.d_s5C = s.din("s5_Ct", [64, 2, 2, 64, 16])
        s.d_s5st = s.din("s5_st0", [64, 2, 2, 64])
        s.d_mask = s.din("w1mask", [128, 2, 128])
        s.d_s5tab = s.nc.dram_tensor("s5tab_scr", [64, 5, 1024], F32).ap()
        s.d_s5A = s.nc.dram_tensor("s5A_scr", [64, 12, 256], F32).ap()
        s.d_cw1 = s.nc.dram_tensor("cw1_scr", [8, 128, 2048], BF16).ap()
        s.d_cw2 = s.nc.dram_tensor("cw2_scr", [8, 128, 2048], BF16).ap()
        s.d_cw4 = s.nc.dram_tensor("cw4_scr", [8, 128, 2048], BF16).ap()
        s.d_yA = s.dout("yT_A", [D, 512])
        s.d_yB = s.dout("yT_B", [D, 1024])
        s.d_ns = s.dout("ns_out", [64, 2, 2, 2, 64])
        if s.stop_after is not None:
            s.d_dbg = s.dout("dbg", [D, 1024])
        s.B = [s.st.enter_context(s.nc.psum_tensor("bank%d" % i, [128, 512], F32)) for i in range(8)]
        s.smallp = s.sb("smallp", [128, NSP])
        s.ident = s.sb("ident", [128, 128])
        s.identb = s.sb("identb", [128, 128], BF16)
        s.onesf = s.sb("onesf", [128, 128])
        s.onesb = s.sb("onesb", [128, 128], BF16)
        s.xT = s.sb("xT", [128, 8, 1024])
        s.hT = s.sb("hT", [128, 8, 1024], BF16)
        s.mod = s.sb("mod", [128, 2, 2, 6, 8])
        s.modA = s.sb("modA", [128, 2, 2, 2, 8])
        s.gb = s.sb("gb", [128, 2, 8])
        s.cond = s.sb("cond", [128, 8, 2])
        s.condb = s.sb("condb", [128, 8, 2], BF16)
        s.dummy = s.sb("dummy", [128, 4])
        s.nscr = s.sb("nscr", [128, 12288 // 4])
        s.ARENA = 143 * 1024
        s.arena = s.sb("arena", [128, s.ARENA // 4])

    def carve(self, off_bytes, shape, dt=F32, parts=128):
        esz = 4 if dt in (F32, I32) else 2
        n = 1
        for d in shape:
            n *= d
        assert off_bytes % 4 == 0
        assert off_bytes + n * esz <= self.ARENA, (off_bytes, shape)
        base = self.arena[0:parts, off_bytes // 4: off_bytes // 4 + (n * esz + 3) // 4]
        if dt != F32:
            base = base.bitcast(dt)
            base = base[:, 0:n]
        if len(shape) == 1:
            return base
        names = " ".join("d%d" % i for i in range(len(shape)))
        kw = {"d%d" % i: shape[i] for i in range(1, len(shape))}
        return base.rearrange("p (%s) -> p %s" % (names, names), **kw)

    def _init_loads(self):
        s = self
        s.init_load(s.smallp[:], s.d_smallp[:, :], "smallp")
        s.init_load(s.ident[:], s.d_ident[:, :], "ident")
        s.init_load(s.identb[:], s.d_ident[:, :], "identb", cast=True)
        s.init_load(s.cond[:], s.d_cond[:, :, :], "cond")
        s.memset("dve", s.onesf[:], 1.0, ["onesf"])
        s.memset("dve", s.onesb[:], 1.0, ["onesb"])

    def spv(self, name, idx, n=1):
        o = SP_OFF[name] + idx
        return self.smallp[:, o:o + n]

    def run_pass(self, ps):
        s = self
        s.ps = ps
        s.NT = 512 if ps == "A" else 1024
        s.L = 256 if ps == "A" else 1024
        s.KT = s.L // 128
        s.seqs = [(0, 256), (256, 512)] if ps == "A" else [(0, 1024)]
        s.ci = 0 if ps == "A" else 1
        s.NTB = s.NT // 512
        s.load_x()
        s.ada()
        if s.stop(ps + "ada"):
            return
        s.hyena_layer()
        if s.stop(ps + "hy"):
            return
        s.ffn(0)
        if s.stop(ps + "ffn0"):
            return
        s.s5_layer()
        if s.stop(ps + "s5y") or s.stop(ps + "s5h"):
            return
        if s.stop(ps + "s5"):
            return
        s.ffn(1)
        if s.stop(ps + "ffn1"):
            return
        s.final_norm()
        s.mark(ps + "final")

    def mark(self, name):
        n = sum(1 for o in self.P.ops if o[0] == "pe")
        self.marks.append((name, n))

    def stop(self, name):
        s = self
        s.mark(name)
        if s.stop_after == name:
            src = s.xT[:, :, 0:s.NT]
            dst = s.d_dbg.rearrange("(k p) t -> p k t", p=128)[:, :, 0:s.NT]
            s.dma("sp", dst, src, "dbg", ["xT"], [])
            return True
        return False

    def load_x(self):
        s = self
        s.barrier()
        NT = s.NT
        src = (s.d_xA if s.ps == "A" else s.d_xB).rearrange("(k p) t -> p k t", p=128)
        s.dma("sp", s.xT[:, :, 0:NT], src, "x_in", [], ["xT"])
        if s.ps == "B":
            psrc = s.d_pos.rearrange("(k p) t -> p k t", p=128)
            xv = s.xT[:, :, 0:NT]
            s.P.add("pool", lambda E: E.dma_start(out=xv, in_=psrc, accum_op=ALU.add), ["xT"], ["xT"], dma="pos0")

    def ada(self):
        s = self
        if getattr(s, "ada_done", False):
            return
        s.ada_done = True
        s.barrier()
        s.act(s.condb[:], s.cond[:], AF.Silu, ["cond"], ["condb"])
        adaw = [s.carve(16384 * i, [8, 1024], BF16) for i in range(3)]
        n = 0
        for i in range(2):
            wsrc = s.d_ada_w[i].rearrange("(k p) n -> p k n", p=128)
            for j in range(6):
                slot = n % 3
                n += 1
                tok = "adaw%d" % slot
                s.dma("pool", adaw[slot][:], wsrc[:, :, j * 1024:(j + 1) * 1024], "ada%d" % slot, [], [tok])
                pb = s.B[7]
                for m in range(8):
                    for k in range(8):
                        s.mm(pb[:, 2 * m:2 * m + 2], adaw[slot][:, k, m * 128:(m + 1) * 128], s.condb[:, k, 0:2],
                             k == 0, k == 7, [tok, "condb"], ["B7"])
                pv = pb[:, 0:16].rearrange("p (m c) -> p c m", c=2)
                for ci in range(2):
                    s.tt("dve", s.mod[:, ci, i, j, :], pv[:, ci, :], s.spv("adab", (i * 6 + j) * 8, 8), ALU.add,
                         ["B7", "smallp"], ["mod"])
        for ci in range(2):
            for i in range(2):
                for w, (gname, jsc) in enumerate([("n1g", 1), ("n2g", 4)]):
                    s.stt("dve", s.modA[:, ci, i, w, :], s.mod[:, ci, i, jsc, :], 1.0, s.spv(gname, i * 8, 8), ALU.add, ALU.mult,
                          ["mod", "smallp"], ["modA"])
            s.tt("dve", s.gb[:, ci, :], s.mod[:, ci, 0, 2, :], s.spv("hyoutb", 0, 8), ALU.mult, ["mod", "smallp"], ["gb"])

    def norm_mod(self, A, Bsh, dst_bf16=None, dst_f32=None):
        s = self
        sq = s.nscr[:, 0:2048].bitcast(BF16).rearrange("p (k t) -> p k t", k=8)
        rstd = s.nscr[:, 2048:2560]
        tmp = s.nscr[:, 2560:3072].rearrange("p (a t) -> p a t", a=1)
        for tb in range(s.NTB):
            tsl = slice(tb * 512, (tb + 1) * 512)
            for k in range(8):
                s.act(sq[:, k, :], s.xT[:, k, tsl], AF.Square, ["xT"], ["nsq"])
            pb = s.B[7]
            for k in range(8):
                s.mm(pb[:, :], s.onesb[:], sq[:, k, :], k == 0, k == 7, ["onesb", "nsq"], ["B7"])
            s.act(rstd[:], pb[:, :], AF.Sqrt, ["B7"], ["nrstd"], bias=EPS, scale=1.0 / D)
            s.recip(rstd[:], rstd[:], ["nrstd"], ["nrstd"])
            for k in range(8):
                if dst_f32 is not None:
                    s.stt("dve", dst_f32[:, k, tsl], s.xT[:, k, tsl], A[:, k:k + 1], rstd[:], ALU.mult, ALU.mult,
                          ["xT", "nrstd", "modA", "smallp"], ["nout"])
                else:
                    t = tmp[:, 0, :]
                    s.stt("dve", t, s.xT[:, k, tsl], A[:, k:k + 1], rstd[:], ALU.mult, ALU.mult,
                          ["xT", "nrstd", "modA", "smallp"], ["ntmp0"])
                    s.act(dst_bf16[:, k, tsl], t, AF.Identity, ["ntmp0", "mod"], ["hT"],
                          bias=Bsh[:, k:k + 1], scale=1.0)

    def ffn(self, layer):
        s = self
        s.barrier()
        NT, NTB = s.NT, s.NTB
        s.norm_mod(s.modA[:, s.ci, layer, 1, :], s.mod[:, s.ci, layer, 3, :], dst_bf16=s.hT)
        g2 = s.mod[:, s.ci, layer, 5, :]
        sT = s.carve(0, [NKF, 1024], BF16)
        o_ = 45056
        w13 = [s.carve(o_ + 4096 * i, [8, 2, 128], BF16) for i in range(3)]
        o_ += 12288
        w13s = [s.carve(o_ + 8192 * i, [8, 2, 128]) for i in range(2)]
        o_ += 16384
        w2b = [s.carve(o_ + 5632 * i, [NKF, 128], BF16) for i in range(2)]
        o_ += 11264
        w2s = [s.carve(o_ + 11264 * i, [NKF, 128]) for i in range(2)]
        o_ += 22528
        stmp = s.carve(o_, [2, 512])
        wsrc = s.d_w13[layer].rearrange("(k p) (h n) -> p k h n", p=128, h=2)
        w2src = s.d_w2[layer].rearrange("(k p) n -> p k n", p=128)

        def load13(m):
            ss = m % 2
            for h_ in range(2):
                s.dma("sp", w13s[ss][:, :, h_, :], wsrc[:, :, h_, m * 128:(m + 1) * 128], "w13s%dh%d" % (ss, h_), [],
                      ["w13s%dh%d" % (ss, h_)])

        def load2(m):
            ss = m % 2
            s.dma("sp", w2s[ss][:], w2src[:, :, m * 128:(m + 1) * 128], "w2s%d" % ss, [], ["w2s%d" % ss])

        load13(0)
        load13(1)
        for m in range(NKF):
            slot = m % 3
            ss = m % 2
            tok = "w13_%d" % slot
            s.cp("act", w13[slot][:], w13s[ss][:], ["w13s%dh0" % ss, "w13s%dh1" % ss], [tok])
            if m + 2 < NKF:
                load13(m + 2)
            elif m + 2 == NKF:
                load2(0)
            elif m + 2 == NKF + 1:
                load2(1)
            for tb in range(NTB):
                tsl = slice(tb * 512, (tb + 1) * 512)
                ba = s.bank(s.B[0:6])
                bb = s.bank(s.B[0:6])
                ta, tbk = "B%d" % s.B.index(ba), "B%d" % s.B.index(bb)
                for k in range(8):
                    s.mm(ba[:, :], w13[slot][:, k, 0, :], s.hT[:, k, tsl], k == 0, k == 7, [tok, "hT"], [ta])
                for k in range(8):
                    s.mm(bb[:, :], w13[slot][:, k, 1, :], s.hT[:, k, tsl], k == 0, k == 7, [tok, "hT"], [tbk])
                st_ = stmp[:, (m * NTB + tb) % 2, :]
                stok = "stmp%d" % ((m * NTB + tb) % 2)
                s.act(st_, ba[:, :], AF.Silu, [ta], [stok])
                s.tt("dve", sT[:, m, tsl], st_, bb[:, :], ALU.mult, [stok, tbk], ["sT"])
        for m in range(8):
            slot = m % 2
            tok = "w2_%d" % slot
            s.cp("act", w2b[slot][:], w2s[slot][:], ["w2s%d" % slot], [tok])
            if m + 2 < 8:
                load2(m + 2)
            for tb in range(NTB):
                tsl = slice(tb * 512, (tb + 1) * 512)
                pb = s.bank(s.B[0:6])
                pt = "B%d" % s.B.index(pb)
                for k in range(NKF):
                    s.mm(pb[:, :], w2b[slot][:, k, :], sT[:, k, tsl], k == 0, k == NKF - 1, [tok, "sT"], [pt])
                s.stt("dve", s.xT[:, m, tsl], pb[:, :], g2[:, m:m + 1], s.xT[:, m, tsl], ALU.mult, ALU.add,
                      [pt, "mod", "xT"], ["xT"])

    def final_norm(self):
        s = self
        s.barrier()
        yo = s.carve(32768, [8, 1024])
        s.norm_mod(s.spv("fing", 0, 8), None, dst_f32=yo)
        dst = (s.d_yA if s.ps == "A" else s.d_yB).rearrange("(k p) t -> p k t", p=128)
        s.dma("sp", dst, yo[:, :, 0:s.NT], "yout", ["nout"], [])

    def hyena_layer(self):
        s = self
        s.barrier()
        ps, NT, NTB, L, KT = s.ps, s.NT, s.NTB, s.L, s.KT
        off = [0]

        def C(shape, dt=F32, parts=128):
            esz = 4 if dt in (F32, I32) else 2
            n = 1
            for d in shape:
                n *= d
            v = s.carve(off[0], shape, dt, parts)
            off[0] += (n * esz + 3) // 4 * 4
            return v

        c4 = C([KT, L], BF16)
        s4 = C([KT, L], BF16)
        hyo = C([8, NT], BF16)
        phi = C([3 * KT])
        h2T = C([L + 1])
        wcb = [C([8, 3, 128], BF16)] * 2
        wo = [C([8, 128], BF16)] * 2
        zq = C([NT])
        vT = C([NT])
        x1T = C([NT])
        x2T = C([NT])
        v2T = C([NT])
        vb = C([NT], BF16)
        vtok = C([NT // 128, 128], BF16)
        Yb = C([2 * KT, 128], BF16)
        Kr = C([KT, 2, 128])
        Ki = C([KT, 2, 128])
        decf = C([KT, 128])
        decb = C([KT, 128])
        w3cb = C([2, 2, 128])
        hd = C([2, 2, 128])
        apb = C([KT, 2, 128], BF16)
        amb = C([KT, 2, 128], BF16)
        rn = C([256])
        arow = C([256])
        off_re = off[0]
        zT = C([L + 1])
        h1T = C([L + 1])
        mtmp = C([512])
        mtmpi = C([512], I32)
        pw1 = C([64])
        pw2 = C([64])
        psm = C([3])
        pq = C([4])
        off[0] = off_re
        cm = [C([512]) for _ in range(4)]
        ft = [C([256]) for _ in range(4)]
        ptmp = C([512])
        habs = ptmp

        s.dma("sp", c4[:], s.d_c4[ps][:, :, :], "init20", [], ["c4"])
        s.dma("sp", s4[:], s.d_s4[ps][:, :, :], "init21", [], ["s4"])
        s.dma("sp", phi[:], s.d_phi[ps][:, :], "init22", [], ["phi"])
        s.dma("sp", zT[0:33, :], s.d_zT[ps][:, :], "init23", [], ["zT"])
        s.dma("sp", pw1[0:33, :], s.d_pe_w1[:, :], "init19", [], ["pw1"])
        s.dma("sp", pw2[0:64, :], s.d_pe_w2[:, :], "init18", [], ["pw2"])
        s.dma("sp", psm[0:64, :], s.d_pe_small[:, :], "init17", [], ["psm"])
        s.norm_mod(s.modA[:, s.ci, 0, 0, :], s.mod[:, s.ci, 0, 0, :], dst_bf16=s.hT)
        for li in range(2):
            s.ts("dve", pq[0:64, 2 * li:2 * li + 1], psm[0:64, 2:3], 1.0 / TWO_PI, None, ALU.mult, None, ["psm"], ["pq"])
            s.stt("dve", pq[0:64, 2 * li + 1:2 * li + 2], psm[0:64, li:li + 1], 1.0 / TWO_PI, psm[0:64, 2:3],
                  ALU.mult, ALU.mult, ["psm"], ["pq"])

        def mlp_layer(src, srcK, stok, w, wtok, dst, dtok, li):
            for c0 in range(0, L + 1, 512):
                n = min(512, L + 1 - c0)
                pb = s.B[6]
                s.mm(pb[0:64, 0:n], w[0:srcK, :], src[0:srcK, c0:c0 + n], True, True, [wtok, stok], ["B6"])
                s.ts("dve", mtmp[0:64, 0:n], pb[0:64, 0:n], pq[0:64, 2 * li:2 * li + 1], pq[0:64, 2 * li + 1:2 * li + 2],
                     ALU.mult, ALU.add, ["B6", "pq"], ["mtmp"])
                s.cp("dve", mtmpi[0:64, 0:n], mtmp[0:64, 0:n], ["mtmp"], ["mtmpi"])
                s.tt("dve", mtmp[0:64, 0:n], mtmp[0:64, 0:n], mtmpi[0:64, 0:n], ALU.subtract, ["mtmp", "mtmpi"], ["mtmp"])
                s.act(dst[0:64, c0:c0 + n], mtmp[0:64, 0:n], AF.Sin, ["mtmp"], [dtok], scale=TWO_PI)

        mlp_layer(zT, 33, "zT", pw1, "pw1", h1T, "h1T", 0)
        mlp_layer(h1T, 64, "h1T", pw2, "pw2", h2T, "h2T", 1)
        s.barrier()
        s.mark(ps + "hy_mlp")

        fb = lambda o, cb: s.spv("fbias", o * 8 + cb)
        nTT = NT // 128
        w3src = s.d_pe_w3.rearrange("j (o d c) -> j d o c", o=2, d=2)
        hsrc = s.d_hyin.rearrange("(k p) (q c) -> p k q c", p=128, q=3)

        for cb in range(8):
            csl = slice(cb * 128, (cb + 1) * 128)
            for q in range(3):
                s.dma("pool", wcb[0][:, :, q, :], hsrc[:, :, q, csl], "hyin%d" % q, [], ["wcb_q%d" % q])
            for d_ in range(2):
                s.dma("sp", w3cb[0:64, d_, :, :], w3src[:, d_, :, csl], "w3cb", [], ["w3cb"])
            s.dma("sp", decf[:], s.d_decf[ps][:, :, csl], "decf", [], ["decf"])
            s.dma("sp", decb[:], s.d_decb[ps][:, :, csl], "decb", [], ["decb"])
            pN = s.B[7]
            for kt in range(KT):
                alt = kt % 2
                pb = s.B[6] if alt == 0 else s.B[5]
                pbt_ = "B6" if alt == 0 else "B5"
                hd_ = hd if alt == 0 else cm[2][:, :].rearrange("p (d o c) -> p d o c", d=2, o=2)
                hdt_ = "hd" if alt == 0 else "cm2"
                hab_ = habs if alt == 0 else cm[3]
                habt_ = "ptmp" if alt == 0 else "cm3"
                s.mm(pb[:, 0:256], h2T[0:64, kt * 128:kt * 128 + 128], w3cb[0:64, 0, :, :], True, True,
                     ["h2T", "w3cb"], [pbt_])
                s.mm(pb[:, 256:512], h2T[0:64, kt * 128 + 1:kt * 128 + 129], w3cb[0:64, 1, :, :], True, True,
                     ["h2T", "w3cb"], [pbt_])
                for d_, dec in enumerate([decf, decb]):
                    dap = dec[:, kt, :]
                    din1 = AP(dap.tensor, dap.offset, [list(dap.ap[0]), [0, 2], [1, 128]])
                    s.tt("dve", hd_[:, d_, :, :], pb[:, d_ * 256:(d_ + 1) * 256].rearrange("p (o c) -> p o c", o=2), din1,
                         ALU.mult, [pbt_, "decf", "decb"], [hdt_])
                s.act(hab_[:], hd_[:].rearrange("p d o c -> p (d o c)"), AF.Abs, [hdt_], [habt_])
                s.mm(pN[:, 0:256], s.onesf[:], hab_[:, 0:256], kt == 0, False, ["onesf", habt_], ["B7"])
                s.mm(pN[:, 0:256], s.onesf[:], hab_[:, 256:512], False, False, ["onesf", habt_], ["B7"])
                s.tt("pool", apb[:, kt, :, :], hd_[:, 0, :, :], hd_[:, 1, :, :], ALU.add, [hdt_], ["apb"])
                s.tt("pool", amb[:, kt, :, :], hd_[:, 0, :, :], hd_[:, 1, :, :], ALU.subtract, [hdt_], ["amb"])
            pb = s.B[6]
            s.mm(pb[0:1, 0:256], h2T[0:64, 0:1], w3cb[0:64, 1, :, :], True, True, ["h2T", "w3cb"], ["B6"])
            s.act(arow[0:1, :], pb[0:1, 0:256], AF.Abs, ["B6"], ["arow"])
            s.mm(pN[:, 0:256], s.onesf[0:1, :], arow[0:1, :], False, True, ["onesf", "arow"], ["B7"])
            s.ts("dve", rn[:], pN[:, 0:256], EPS, float(L), ALU.add, ALU.mult, ["B7"], ["rn"])
            s.recip(rn[:], rn[:], ["rn"], ["rn"])
            s.mark(ps + "hy_cb%d_filt" % cb)
            slot = cb % 2
            wtok = "wcb0"
            c2_ = cm[2][:, :]
            zq_alt = AP(c2_.tensor, c2_.offset, [list(c2_.ap[0]), [1, NT]])
            for q, dst in enumerate([vT, x1T, x2T]):
                fi = q * 8 + cb
                zq_ = zq if q != 1 else zq_alt
                zt_ = ["zq"] if q != 1 else ["cm2", "cm3"]
                for tb in range(NTB):
                    tsl = slice(tb * 512, (tb + 1) * 512)
                    pb = s.bank(s.B[4:6])
                    pt = "B%d" % s.B.index(pb)
                    for k in range(8):
                        s.mm(pb[:, :], wcb[slot][:, k, q, :], s.hT[:, k, tsl], k == 0, k == 7, ["wcb_q%d" % q, "hT"], [pt])
                    s.act(zq_[:, tsl], pb[:, :], AF.Identity, [pt, "smallp"], zt_, bias=s.spv("hyinb", fi), scale=1.0)
                dtok = ["vT", "x1T", "x2T"][q]
                eng = "dve"
                s.act(dst[:], zq_[:, 0:NT], AF.Identity, zt_ + ["smallp"], [dtok], bias=s.spv("convb", fi), scale=s.spv("convw", 24 + fi))
                for (a, e) in s.seqs:
                    s.stt(eng, dst[:, a + 1:e], zq_[:, a:e - 1], s.spv("convw", fi), dst[:, a + 1:e], ALU.mult, ALU.add,
                          zt_ + [dtok, "smallp"], [dtok])
                    s.stt(eng, dst[:, a:e - 1], zq_[:, a + 1:e], s.spv("convw", 48 + fi), dst[:, a:e - 1], ALU.mult, ALU.add,
                          zt_ + [dtok, "smallp"], [dtok])
            s.cp("act", vb[:], vT[:], ["vT"], ["vb"])

            def sp_mm(m):
                pb = s.bank(s.B[4:6])
                pt = "B%d" % s.B.index(pb)
                for kt in range(KT):
                    s.mm(pb[:, 0:256], c4[:, kt, m * 128:(m + 1) * 128], apb[:, kt, :, :], kt == 0, kt == KT - 1,
                         ["c4", "apb"], [pt])
                for kt in range(KT):
                    s.mm(pb[:, 256:512], s4[:, kt, m * 128:(m + 1) * 128], amb[:, kt, :, :], kt == 0, kt == KT - 1,
                         ["s4", "amb"], [pt])
                return pb, pt

            def sp_bufs(m):
                if m % 2 == 0:
                    return [ft[i][:] for i in range(4)], ["ft0", "ft1", "ft2", "ft3"]
                return ([cm[0][:, 0:256], cm[0][:, 256:512], cm[1][:, 0:256], cm[1][:, 256:512]],
                        ["cm0", "cm0", "cm1", "cm1"])

            def sp_scale(m, pb, pt):
                f_, t_ = sp_bufs(m)
                s.tt("dve", f_[0], pb[:, 0:256], rn[:], ALU.mult, [pt, "rn"], [t_[0]])
                s.tt("dve", f_[1], pb[:, 256:512], rn[:], ALU.mult, [pt, "rn"], [t_[1]])

            def sp_rot(m):
                f_, t_ = sp_bufs(m)
                cph = phi[:, m:m + 1]
                sph = phi[:, KT + m:KT + m + 1]
                nsph = phi[:, 2 * KT + m:2 * KT + m + 1]
                s.act(f_[2], f_[0], AF.Identity, [t_[0], "phi"], [t_[2]], scale=cph)
                s.act(f_[3], f_[0], AF.Identity, [t_[0], "phi"], [t_[3]], scale=sph)
                s.stt("dve", Kr[:, m, :, :].rearrange("p o c -> p (o c)"), f_[1], nsph, f_[2], ALU.mult, ALU.add,
                      [t_[1], t_[2], "phi"], ["Kr"])
                s.stt("dve", Ki[:, m, :, :].rearrange("p o c -> p (o c)"), f_[1], cph, f_[3], ALU.mult, ALU.add,
                      [t_[1], t_[3], "phi"], ["Ki"])

            pbm = sp_mm(0)
            sp_scale(0, *pbm)
            for m in range(KT):
                if m + 1 < KT:
                    pbn = sp_mm(m + 1)
                    sp_scale(m + 1, *pbn)
                sp_rot(m)

            s.mark(ps + "hy_cb%d_in" % cb)
            for o in range(2):
                vsrc = vT if o == 0 else v2T
                vstok = "vT" if o == 0 else "v2T"
                gate = x1T if o == 0 else x2T
                gtok = "x1T" if o == 0 else "x2T"
                pbt = s.B[6]
                for tt_ in range(nTT):
                    bsel = pbt if tt_ < 4 else s.B[7]
                    btok = "B6" if tt_ < 4 else "B7"
                    s.mm(bsel[:, (tt_ % 4) * 128:(tt_ % 4 + 1) * 128], vb[:, tt_ * 128:(tt_ + 1) * 128], s.identb[:],
                         True, True, ["vb", "identb"], [btok])
                s.cp("act", vtok[:, 0:min(4, nTT), :].rearrange("p a c -> p (a c)"), pbt[:, 0:128 * min(4, nTT)],
                     ["B6"], ["vtok"])
                if nTT > 4:
                    s.cp("act", vtok[:, 4:8, :].rearrange("p a c -> p (a c)"), s.B[7][:, :], ["B7"], ["vtok"])
                for (a, e) in s.seqs:
                    tt0 = a // 128
                    HF = min(4, KT)
                    for half in range(KT // HF):
                        pr, pi = (s.B[0], s.B[1]) if half % 2 == 0 else (s.B[2], s.B[3])
                        tr, ti = ("B0", "B1") if half % 2 == 0 else ("B2", "B3")
                        for mi in range(HF):
                            m = half * HF + mi
                            for kt in range(KT):
                                s.mm(pr[:, mi * 128:(mi + 1) * 128], c4[:, kt, m * 128:(m + 1) * 128], vtok[:, tt0 + kt, :],
                                     kt == 0, kt == KT - 1, ["c4", "vtok"], [tr])
                            for kt in range(KT):
                                s.mm(pi[:, mi * 128:(mi + 1) * 128], s4[:, kt, m * 128:(m + 1) * 128], vtok[:, tt0 + kt, :],
                                     kt == 0, kt == KT - 1, ["s4", "vtok"], [ti])
                        W = HF * 128
                        msl = slice(half * HF, half * HF + HF)
                        vr = pr[:, 0:W].rearrange("p (m c) -> p m c", c=128)
                        vi = pi[:, 0:W].rearrange("p (m c) -> p m c", c=128)
                        c0 = cm[0][:, 0:W].rearrange("p (m c) -> p m c", c=128)
                        c1 = cm[1][:, 0:W].rearrange("p (m c) -> p m c", c=128)
                        c2 = cm[2][:, 0:W].rearrange("p (m c) -> p m c", c=128)
                        c3 = cm[3][:, 0:W].rearrange("p (m c) -> p m c", c=128)
                        s.tt("dve", c0, vr, Kr[:, msl, o, :], ALU.mult, [tr, "Kr"], ["cm0"])
                        s.tt("dve", c1, vi, Ki[:, msl, o, :], ALU.mult, [ti, "Ki"], ["cm1"])
                        s.tt("pool", Yb[:, msl, :], c0, c1, ALU.subtract, ["cm0", "cm1"], ["Yb"])
                        s.tt("dve", c2, vr, Ki[:, msl, o, :], ALU.mult, [tr, "Ki"], ["cm2"])
                        s.tt("dve", c3, vi, Kr[:, msl, o, :], ALU.mult, [ti, "Kr"], ["cm3"])
                        s.tt("pool", Yb[:, KT + half * HF:KT + half * HF + HF, :], c2, c3, ALU.add, ["cm2", "cm3"], ["Yb"])
                    NB = min(512, L)
                    for tblk in range(L // NB):
                        tsl = slice(a + tblk * NB, a + (tblk + 1) * NB)
                        fsl = slice(tblk * NB, (tblk + 1) * NB)
                        pb = s.bank(s.B[4:6])
                        pt = "B%d" % s.B.index(pb)
                        for kf in range(KT):
                            s.mm(pb[:, 0:NB], Yb[:, kf, :], c4[:, kf, fsl], kf == 0, False, ["Yb", "c4"], [pt])
                        for kf in range(KT):
                            s.mm(pb[:, 0:NB], Yb[:, KT + kf, :], s4[:, kf, fsl], False, kf == KT - 1, ["Yb", "s4"], [pt])
                        s.stt("dve", ptmp[:, 0:NB], vsrc[:, tsl], fb(o, cb), pb[:, 0:NB], ALU.mult, ALU.add,
                              [vstok, pt, "smallp"], ["ptmp"])
                        if o == 0:
                            s.tt("pool", v2T[:, tsl], ptmp[:, 0:NB], gate[:, tsl], ALU.mult, ["ptmp", gtok], ["v2T"])
                        else:
                            s.tt("pool", hyo[:, cb, tsl], ptmp[:, 0:NB], gate[:, tsl], ALU.mult, ["ptmp", gtok], ["hyo"])
                if o == 0:
                    s.cp("act", vb[:], v2T[:], ["v2T"], ["vb"])
                s.mark(ps + "hy_cb%d_o%d" % (cb, o))

        g1 = s.mod[:, s.ci, 0, 2, :]
        osrc = s.d_hyout.rearrange("(k p) n -> p k n", p=128)
        wo_alt = cm[0][:, :].bitcast(BF16)[:, 0:1024].rearrange("p (k c) -> p k c", k=8)
        wos = [(wo[0], "wo0", "hyout0"), (wo_alt, "cm0", "hyout1")]
        s.dma("pool", wos[0][0][:], osrc[:, :, 0:128], wos[0][2], [], [wos[0][1]])
        for m in range(8):
            slot = m % 2
            tok = wos[slot][1]
            if m + 1 < 8:
                nx = wos[(m + 1) % 2]
                s.dma("pool", nx[0][:], osrc[:, :, (m + 1) * 128:(m + 2) * 128], nx[2], [], [nx[1]])
            for tb in range(NTB):
                tsl = slice(tb * 512, (tb + 1) * 512)
                pb = s.bank(s.B[0:6])
                pt = "B%d" % s.B.index(pb)
                for k in range(8):
                    s.mm(pb[:, :], wos[slot][0][:, k, :], hyo[:, k, tsl], k == 0, k == 7, [tok, "hyo"], [pt])
                s.stt("dve", s.xT[:, m, tsl], pb[:, :], g1[:, m:m + 1], s.xT[:, m, tsl], ALU.mult, ALU.add,
                      [pt, "mod", "xT"], ["xT"])
            s.ts("dve", s.xT[:, m, 0:NT], s.xT[:, m, 0:NT], s.gb[:, s.ci, m:m + 1], None, ALU.add, None, ["xT", "gb"], ["xT"])

    def s5_prep(self):
        s = self
        s.barrier()
        off = [0]

        def C(shape, dt=F32, parts=128):
            esz = 4 if dt in (F32, I32) else 2
            n = 1
            for d in shape:
                n *= d
            v = s.carve(off[0], shape, dt, parts)
            off[0] += (n * esz + 3) // 4 * 4
            return v

        s5a = C([3, 128])
        s5e = C([2, 2, 8])
        cmpA = C([19, 128])
        Aout = C([12, 2, 2, 64])
        tb_ = C([15, 1024])
        s.dma("sp", s5a[0:64, :, :], s.d_s5a[:, :, :], "init16", [], ["s5a"])
        s.dma("sp", s5e[0:64, :, :, :].rearrange("p w d j -> p (w d j)"), s.d_s5e[:, :], "init15", [], ["s5e"])

        def cA(i):
            return cmpA[0:64, i, :]
        are, aim, ldt = s5a[0:64, 0, :], s5a[0:64, 1, :], s5a[0:64, 2, :]
        dt_, er, ph, mag, y_, yi_, sn, cs, lr, li = [cA(i) for i in range(10)]
        nr, den, cor, coi, Rr, Ri, t0_, t1_ = [cA(i) for i in range(10, 18)]
        yiI = cmpA[0:64, 18, :].bitcast(I32)

        def sincos(phase, scale, sn_o, cs_o, y, yI, toks_in):
            for (o, add) in ((sn_o, 0.0), (cs_o, 0.25)):
                s.ts("dve", y, phase, scale / TWO_PI, add, ALU.mult, ALU.add, toks_in, ["sc_y"])
                s.cp("dve", yI, y, ["sc_y"], ["sc_yi"])
                s.tt("dve", y, y, yI, ALU.subtract, ["sc_y", "sc_yi"], ["sc_y"])
                s.act(o, y, AF.Sin, ["sc_y"], ["sc_o"], scale=TWO_PI)

        s.act(dt_, ldt, AF.Exp, ["s5a"], ["cmp"])
        s.tt("dve", er, dt_, are, ALU.mult, ["cmp", "s5a"], ["cmp"])
        s.tt("dve", ph, dt_, aim, ALU.mult, ["cmp", "s5a"], ["cmp"])
        s.act(mag, er, AF.Exp, ["cmp"], ["cmp"])
        sincos(ph, 1.0, sn, cs, y_, yiI, ["cmp"])
        s.tt("dve", lr, mag, cs, ALU.mult, ["cmp", "sc_o"], ["cmp"])
        s.tt("dve", li, mag, sn, ALU.mult, ["cmp", "sc_o"], ["cmp"])
        s.ts("dve", nr, lr, -1.0, None, ALU.add, None, ["cmp"], ["cmp"])
        s.tt("dve", t0_, are, are, ALU.mult, ["s5a"], ["cmp"])
        s.tt("dve", den, aim, aim, ALU.mult, ["s5a"], ["cmp"])
        s.tt("dve", den, den, t0_, ALU.add, ["cmp"], ["cmp"])
        s.recip(den, den, ["cmp"], ["cmp"])
        s.tt("dve", t0_, nr, are, ALU.mult, ["cmp", "s5a"], ["cmp"])
        s.tt("dve", t1_, li, aim, ALU.mult, ["cmp", "s5a"], ["cmp"])
        s.tt("dve", t0_, t0_, t1_, ALU.add, ["cmp"], ["cmp"])
        s.tt("dve", cor, t0_, den, ALU.mult, ["cmp"], ["cmp"])
        s.tt("dve", t0_, li, are, ALU.mult, ["cmp", "s5a"], ["cmp"])
        s.tt("dve", t1_, nr, aim, ALU.mult, ["cmp", "s5a"], ["cmp"])
        s.tt("dve", t0_, t0_, t1_, ALU.subtract, ["cmp"], ["cmp"])
        s.tt("dve", coi, t0_, den, ALU.mult, ["cmp"], ["cmp"])
        s.act(mag, er, AF.Exp, ["cmp"], ["cmp"], scale=8.0)
        sincos(ph, 8.0, sn, cs, y_, yiI, ["cmp"])
        s.tt("dve", Rr, mag, cs, ALU.mult, ["cmp", "sc_o"], ["cmp"])
        s.tt("dve", Ri, mag, sn, ALU.mult, ["cmp", "sc_o"], ["cmp"])
        r8r, r8i = cA(8), cA(9)
        s.cp("dve", r8r, Rr, ["cmp"], ["cmp"])
        s.cp("dve", r8i, Ri, ["cmp"], ["cmp"])
        for _sq in range(3):
            s.tt("dve", t0_, r8r, r8r, ALU.mult, ["cmp"], ["cmp"])
            s.tt("dve", t1_, r8i, r8i, ALU.mult, ["cmp"], ["cmp"])
            s.tt("dve", y_, r8r, r8i, ALU.mult, ["cmp"], ["cmp"])
            s.tt("dve", r8r, t0_, t1_, ALU.subtract, ["cmp"], ["cmp"])
            s.ts("dve", r8i, y_, 2.0, None, ALU.mult, None, ["cmp"], ["cmp"])
        def emitA(ai, vr, vi):
            vrV = vr.rearrange("p (d g) -> p d g", d=2)
            viV = vi.rearrange("p (d g) -> p d g", d=2)
            for ri in range(2):
                s.cp("dve", Aout[0:64, 2 * ai, :, ri, :], vrV, ["cmp"], ["Aout"])
            s.ts("dve", Aout[0:64, 2 * ai + 1, :, 0, :], viV, -1.0, None, ALU.mult, None, ["cmp"], ["Aout"])
            s.cp("dve", Aout[0:64, 2 * ai + 1, :, 1, :], viV, ["cmp"], ["Aout"])

        emitA(0, Rr, Ri)
        emitA(1, r8r, r8i)
        for lv in range(1, 5):
            s.tt("dve", t0_, r8r, r8r, ALU.mult, ["cmp"], ["cmp"])
            s.tt("dve", t1_, r8i, r8i, ALU.mult, ["cmp"], ["cmp"])
            s.tt("dve", y_, r8r, r8i, ALU.mult, ["cmp"], ["cmp"])
            s.tt("dve", r8r, t0_, t1_, ALU.subtract, ["cmp"], ["cmp"])
            s.ts("dve", r8i, y_, 2.0, None, ALU.mult, None, ["cmp"], ["cmp"])
            emitA(1 + lv, r8r, r8i)
        s.dma("sp", s.d_s5A[:, :, :], Aout[0:64, :, :, :, :].rearrange("p a d r g -> p a (d r g)"), "s5A", ["Aout"], ["d_s5A"])

        erV = er.rearrange("p (d g) -> p d g", d=2)
        phV = ph.rearrange("p (d g) -> p d g", d=2)
        corV = cor.rearrange("p (d g) -> p d g", d=2)
        coiV = coi.rearrange("p (d g) -> p d g", d=2)

        def bc_last(ap3, n):
            a = ap3.ap
            return AP(ap3.tensor, ap3.offset, [list(a[0]), list(a[1]), list(a[2]), [0, n]])

        def T(i):
            return tb_[0:64, i, :].rearrange("p (d g j) -> p d g j", d=2, g=64)

        outs = {}
        for w in range(2):
            Ev = s5e[0:64, w, :, :]
            Eb = AP(Ev.tensor, Ev.offset, [list(Ev.ap[0]), list(Ev.ap[1]), [0, 64], list(Ev.ap[2])])
            earg, mg, yy, pr, pi_, sne, cse = T(0), T(1), T(2), T(3 + 4 * w), T(4 + 4 * w), T(11), T(12)
            yyI = tb_[0:64, 13, :].bitcast(I32).rearrange("p (d g j) -> p d g j", d=2, g=64)
            s.tt("dve", earg, Eb, bc_last(erV, 8), ALU.mult, ["s5e", "cmp"], ["tg"])
            s.act(mg, earg, AF.Exp, ["tg"], ["tg"])
            s.tt("dve", earg, Eb, bc_last(phV, 8), ALU.mult, ["s5e", "cmp"], ["tg"])
            for (o, add) in ((sne, 0.0), (cse, 0.25)):
                s.ts("dve", yy, earg, 1.0 / TWO_PI, add, ALU.mult, ALU.add, ["tg"], ["tg"])
                s.cp("dve", yyI, yy, ["tg"], ["tg"])
                s.tt("dve", yy, yy, yyI, ALU.subtract, ["tg"], ["tg"])
                s.act(o, yy, AF.Sin, ["tg"], ["tg"], scale=TWO_PI)
            s.tt("dve", pr, mg, cse, ALU.mult, ["tg"], ["tg"])
            s.tt("dve", pi_, mg, sne, ALU.mult, ["tg"], ["tg"])
            if w == 0:
                mr, mi = T(5), T(6)
                cr_b, ci_b = bc_last(corV, 8), bc_last(coiV, 8)
                s.tt("dve", T(0), pr, cr_b, ALU.mult, ["tg", "cmp"], ["tg"])
                s.tt("dve", T(1), pi_, ci_b, ALU.mult, ["tg", "cmp"], ["tg"])
                s.tt("dve", mr, T(0), T(1), ALU.subtract, ["tg"], ["tg"])
                s.tt("dve", T(0), pr, ci_b, ALU.mult, ["tg", "cmp"], ["tg"])
                s.tt("dve", T(1), pi_, cr_b, ALU.mult, ["tg", "cmp"], ["tg"])
                s.tt("dve", mi, T(0), T(1), ALU.add, ["tg"], ["tg"])
                outs[0], outs[1] = 5, 6
            else:
                s.ts("dve", T(10), pi_, -1.0, None, ALU.mult, None, ["tg"], ["tg"])
                outs[2], outs[3], outs[4] = 7, 8, 10
        for t_i in range(5):
            s.dma("sp", s.d_s5tab[:, t_i, :], tb_[0:64, outs[t_i], :], "scrw%d" % (t_i % 2), ["tg"], ["d_s5tab"])

    def s5_layer(self):
        s = self
        s.barrier()
        ps, NT, NTB = s.ps, s.NT, s.NTB
        CH = NT // 8
        nseq = len(s.seqs)
        Cl = CH // nseq
        s.norm_mod(s.modA[:, s.ci, 1, 0, :], s.mod[:, s.ci, 1, 0, :], dst_bf16=s.hT)
        uT = s.hT
        off = [0]

        def C(shape, dt=F32, parts=128):
            esz = 4 if dt in (F32, I32) else 2
            n = 1
            for d in shape:
                n *= d
            v = s.carve(off[0], shape, dt, parts)
            off[0] += (n * esz + 3) // 4 * 4
            return v

        XY = C([8, 1024], BF16)
        UY = C([8, NT], BF16)
        Ush = UY.rearrange("p k t -> p (k t)").rearrange("p (g c) -> p g c", c=CH)
        Aall = C([12, 2, 64])
        A1, A2 = Aall[:, 0, :, :], Aall[:, 1, :, :]
        AL = [(Aall[:, 2 + 2 * l, :, :], Aall[:, 3 + 2 * l, :, :]) for l in range(5)]
        tabs_t = C([5, 8, 8])
        mask = C([2, 128])
        NS = C([2, 2, 64]) if ps == "A" else None
        st0 = C([2, 64]) if ps == "B" else None
        off_glu = off[0]
        gen_here = not (ps == "B" and getattr(s, "s5w_cached", False))
        if gen_here:
            Bin = C([2, 8, 16])
            Cin = C([2, 8, 16])
            X1 = C([2, 8, 128], BF16)
            tA = [C([512]) for _ in range(2)]
            tB = [C([512]) for _ in range(2)]
        W4p = [C([2, 8, 128], BF16) for _ in range(2)]
        W2p = [C([2, 8, 128], BF16) for _ in range(2)]
        W1p = [C([2, 8, 128], BF16) for _ in range(2)]
        NB = Cl // 8
        QB = nseq * NB
        SBp = [C([8, 2, 8, QB]) for _ in range(2)]
        HBhp = [C([2, 8, nseq, Cl], BF16) for _ in range(2)]
        Pbp = [C([8, 2, 8, QB]) for _ in range(2)]
        HS = C([2, 8, nseq, NB + 1])
        Qb = C([2, 8, QB])
        rT = C([2, 8, QB])
        rU = C([2, 8, QB])
        rV = C([2, 8, QB])
        Ysh = C([8, CH], BF16)

        ib = s.identb
        asrc = s.d_s5A.rearrange("p a (d x) -> p a d x", d=2)
        for d_ in range(2):
            s.dma("sp", Aall[d_ * 64:(d_ + 1) * 64, :, :, :].rearrange("p a r g -> p a (r g)"), asrc[:, :, d_, :],
                  "s5A" if d_ == 0 else "s5A1", ["d_s5A"], ["A12"])
        s.dma("sp", mask[:], s.d_mask[:, :, :], "init14", [], ["mask"])
        if ps == "B":
            for d_ in range(2):
                s.dma("sp", st0[d_ * 64:(d_ + 1) * 64, :, :], s.d_s5st[:, d_, :, :], "s5st" if d_ == 0 else "s5st1", [], ["st0"])

        uv = [uT[:, k, 0:NT].rearrange("p (c j) -> p j c", j=8) for k in range(8)]
        for k in range(8):
            for jh in range(2):
                pb = s.bank(s.B[0:4])
                pt = "B%d" % s.B.index(pb)
                for jj in range(4):
                    j = jh * 4 + jj
                    s.mm(pb[0:CH, jj * 128:(jj + 1) * 128], uv[k][:, j, :], ib[:, :], True, True, ["hT", "identb"], [pt])
                eng = "dve" if (k * 2 + jh) % 2 == 0 else "act"
                xv = XY[0:CH, :, :]
                rs = xv.ap[1][0] * 8 // 8
                dst4 = AP(xv.tensor, xv.offset + k * 8 * 128 + jh * 4 * 16, [list(xv.ap[0]), [16, 4], [128, 8], [1, 16]])
                s.cp(eng, dst4, pb[0:CH, :].rearrange("p (a g h) -> p a g h", a=4, g=8), [pt], ["XY"])
        s.mark(ps + "s5_shufA")
        GPB = 512 // CH
        for g0 in range(0, 64, GPB):
            pb = s.bank(s.B[0:4])
            pt = "B%d" % s.B.index(pb)
            for gi in range(GPB):
                g = g0 + gi
                xg = XY[0:CH, :, :].rearrange("p j f -> p (j f)")[:, g * 128:(g + 1) * 128]
                s.mm(pb[:, gi * CH:(gi + 1) * CH], xg, ib[0:CH, 0:CH], True, True, ["XY", "identb"], [pt])
            eng = "dve" if (g0 // GPB) % 2 == 0 else "act"
            s.cp(eng, Ush[:, g0:g0 + GPB, :], pb[:, :].rearrange("p (a c) -> p a c", a=GPB), [pt], ["UY"])

        if ps == "A":
            s.memset("pool", HS[:, :, :, :, :], 0.0, ["HS"])

        def wgen(gb, par):
            gsl = slice(gb * 8, (gb + 1) * 8)
            W1, W2, W4 = W1p[par], W2p[par], W4p[par]
            t1_, t2_, t4_ = "W1_%d" % par, "W2_%d" % par, "W4_%d" % par
            if not gen_here:
                s.dma("sp", W1[:, :, :, :].rearrange("p d g q -> p (d g q)"), s.d_cw1[gb], "cw1_%d" % par, ["d_cw%d" % gb], [t1_])
                s.dma("sp", W2[:, :, :, :].rearrange("p r g q -> p (r g q)"), s.d_cw2[gb], "cw2_%d" % par, ["d_cw%d" % gb], [t2_])
                s.dma("sp", W4[:, :, :, :].rearrange("p r g q -> p (r g q)"), s.d_cw4[gb], "cw4_%d" % par, ["d_cw%d" % gb], [t4_])
                return
            tsrc = s.d_s5tab.rearrange("p t (d g j) -> p t d (g j)", d=2, g=64)
            for d_ in range(2):
                ph = slice(d_ * 64, (d_ + 1) * 64)
                for ri in range(2):
                    s.dma("sp", Bin[ph, ri, :, :], s.d_s5B[:, ri, d_, gsl, :], "s5b" if d_ == 0 else "s5b1", [], ["Bin"])
                    s.dma("sp", Cin[ph, ri, :, :], s.d_s5C[:, ri, d_, gsl, :], "s5c" if d_ == 0 else "s5c1", [], ["Cin"])
                s.dma("sp", tabs_t[ph, :, :, :].rearrange("p t g j -> p t (g j)"), tsrc[:, :, d_, gb * 64:(gb + 1) * 64],
                      "s5t%d" % d_, ["d_s5tab"], ["tabM", "tabL"])
            tabs = {"Mr": tabs_t[:, 0], "Mi": tabs_t[:, 1], "Lr": tabs_t[:, 2], "Li": tabs_t[:, 3], "nLi": tabs_t[:, 4]}

            def tab_b(tv, gs_):
                v = tv[:, gs_, :]
                return AP(v.tensor, v.offset, [list(v.ap[0]), list(v.ap[1]), list(v.ap[2]), [0, 16]])

            def in_b(tile_, ri, gs_):
                v = tile_[:, ri, gs_, :]
                return AP(v.tensor, v.offset, [list(v.ap[0]), list(v.ap[1]), [0, 8], list(v.ap[2])])

            def o4(tile_, ri, gs_):
                return tile_[:, ri, gs_, :].rearrange("p g (j h) -> p g j h", j=8)

            def t4(tl):
                return tl[:, :].rearrange("p (g j h) -> p g j h", g=4, j=8)

            for g2 in range(2):
                gs_ = slice(g2 * 4, g2 * 4 + 4)
                e = "dve"
                s.tt(e, t4(tA[0]), tab_b(tabs["Mr"], gs_), in_b(Bin, 0, gs_), ALU.mult, ["tabM", "Bin"], ["tA0"])
                s.tt(e, t4(tA[1]), tab_b(tabs["Mi"], gs_), in_b(Bin, 1, gs_), ALU.mult, ["tabM", "Bin"], ["tA1"])
                s.tt(e, o4(X1, 0, gs_), t4(tA[0]), t4(tA[1]), ALU.subtract, ["tA0", "tA1"], ["X1"])
                s.tt(e, t4(tA[0]), tab_b(tabs["Mr"], gs_), in_b(Bin, 1, gs_), ALU.mult, ["tabM", "Bin"], ["tA0"])
                s.tt(e, t4(tA[1]), tab_b(tabs["Mi"], gs_), in_b(Bin, 0, gs_), ALU.mult, ["tabM", "Bin"], ["tA1"])
                s.tt(e, o4(X1, 1, gs_), t4(tA[0]), t4(tA[1]), ALU.add, ["tA0", "tA1"], ["X1"])
                e = "pool"
                s.tt(e, t4(tB[0]), tab_b(tabs["Lr"], gs_), in_b(Cin, 0, gs_), ALU.mult, ["tabL", "Cin"], ["tB0"])
                s.tt(e, t4(tB[1]), tab_b(tabs["Li"], gs_), in_b(Cin, 1, gs_), ALU.mult, ["tabL", "Cin"], ["tB1"])
                s.tt(e, o4(W4, 0, gs_), t4(tB[0]), t4(tB[1]), ALU.subtract, ["tB0", "tB1"], [t4_])
                s.tt(e, t4(tB[0]), tab_b(tabs["nLi"], gs_), in_b(Cin, 0, gs_), ALU.mult, ["tabL", "Cin"], ["tB0"])
                s.tt(e, t4(tB[1]), tab_b(tabs["Lr"], gs_), in_b(Cin, 1, gs_), ALU.mult, ["tabL", "Cin"], ["tB1"])
                s.tt(e, o4(W4, 1, gs_), t4(tB[0]), t4(tB[1]), ALU.subtract, ["tB0", "tB1"], [t4_])
            for ri in range(2):
                for gh in range(2):
                    pb = s.bank(s.B[0:4])
                    pt = "B%d" % s.B.index(pb)
                    for gi in range(4):
                        g = gh * 4 + gi
                        s.mm(pb[:, gi * 128:(gi + 1) * 128], X1[:, ri, g, :], ib[:, :], True, True, ["X1", "identb"], [pt])
                    s.cp("act", W2[:, ri, gh * 4:gh * 4 + 4, :], pb[:, :].rearrange("p (g q) -> p g q", g=4), [pt], [t2_])
            for d_ in range(2):
                ph = slice(d_ * 64, (d_ + 1) * 64)
                for gh in range(2):
                    pb = s.bank(s.B[0:4])
                    pt = "B%d" % s.B.index(pb)
                    for gi in range(4):
                        g = gh * 4 + gi
                        s.mm(pb[:, gi * 128:(gi + 1) * 128], X1[ph, 0, g, :], W4[ph, 0, g, :], True, False, ["X1", t4_], [pt])
                        s.mm(pb[:, gi * 128:(gi + 1) * 128], X1[ph, 1, g, :], W4[ph, 1, g, :], False, True, ["X1", t4_], [pt])
                    mv = mask[:, d_, :]
                    mb = AP(mv.tensor, mv.offset, [list(mv.ap[0]), [0, 4], list(mv.ap[1])])
                    s.tt("dve", W1[:, d_, gh * 4:gh * 4 + 4, :], pb[:, :].rearrange("p (g q) -> p g q", g=4), mb, ALU.mult,
                         [pt, "mask"], [t1_])
            if ps == "A":
                s.dma("sp", s.d_cw1[gb], W1[:, :, :, :].rearrange("p d g q -> p (d g q)"), "cw1_%d" % par, [t1_], ["d_cw%d" % gb])
                s.dma("sp", s.d_cw2[gb], W2[:, :, :, :].rearrange("p r g q -> p (r g q)"), "cw2_%d" % par, [t2_], ["d_cw%d" % gb])
                s.dma("sp", s.d_cw4[gb], W4[:, :, :, :].rearrange("p r g q -> p (r g q)"), "cw4_%d" % par, [t4_], ["d_cw%d" % gb])

        def bc3(a_, n_):
            return AP(a_.tensor, a_.offset, [list(x) for x in a_.ap] + [[0, n_]])


        def ctx(gb):
            par = gb % 2
            return (slice(gb * 8, (gb + 1) * 8), par, W1p[par], W2p[par], W4p[par], "W1_%d" % par, "W2_%d" % par, "W4_%d" % par,
                    SBp[par], Pbp[par], HBhp[par], "SB%d" % par, "P%d" % par, "HBh%d" % par)

        def st_t2(gb):
            gsl, par, W1, W2, W4, tW1, tW2, tW4, SB, Pb, HBh, tSB, tP, tHB = ctx(gb)
            for ri in range(2):
                for g0 in range(0, 8, GPB):
                    pb = s.bank(s.B[0:4])
                    pt = "B%d" % s.B.index(pb)
                    for gi in range(GPB):
                        g = g0 + gi
                        s.mm(pb[:, gi * CH:(gi + 1) * CH], W2[:, ri, g, :], Ush[:, gb * 8 + g, :], True, True, [tW2, "UY"], [pt])
                    s.cp("act", SB[0:64, :, ri, g0:g0 + GPB, :], pb[0:64, :].rearrange("p (g x i) -> p i g x", g=GPB, i=8),
                         [pt], [tSB])
                    for q in range(nseq):
                        ov = SB[64:128, :, ri, g0:g0 + GPB, q * NB:(q + 1) * NB]
                        oa = ov.ap
                        orev = AP(ov.tensor, ov.offset + 7 * oa[1][0] + (NB - 1) * oa[3][0],
                                  [list(oa[0]), [-oa[1][0], 8], list(oa[2]), [-oa[3][0], NB]])
                        iv = pb[64:128, :].rearrange("p (g x i) -> p i g x", g=GPB, i=8)[:, :, :, q * NB:(q + 1) * NB]
                        s.cp("act", orev, iv, [pt], [tSB])

        def st_rec(gb):
            gsl, par, W1, W2, W4, tW1, tW2, tW4, SB, Pb, HBh, tSB, tP, tHB = ctx(gb)
            if ps == "B":
                s.cp("act", HS[:, :, :, 0, 0], st0[:, :, gsl], ["st0"], ["HS"])
            for ai_, asrc_ in enumerate((A1, A2)):
                pass

            def cmul(out, otok, in_all, itoks, a1, a2, nb_):
                a1b_ = bc3(a1[:, :, gsl], nb_)
                a2b_ = bc3(a2[:, :, gsl], nb_)
                uu, vv = rU[:, :, :, 0:nb_], rV[:, :, :, 0:nb_]
                ia = in_all.ap
                insw = AP(in_all.tensor, in_all.offset + ia[1][0], [list(ia[0]), [-ia[1][0], 2]] + [list(x) for x in ia[2:]])
                s.tt(e, uu, in_all, a1b_, ALU.mult, itoks + ["A12"], ["rU"])
                s.tt(e, vv, insw, a2b_, ALU.mult, itoks + ["A12"], ["rV"])
                s.tt(e, out, uu, vv, ALU.add, ["rU", "rV"], [otok])

            for k in range(8):
                if k == 0:
                    cmul(Pb[:, 0], tP, SB[:, 0], [tSB], A1, A2, QB)
                else:
                    s.tt(e, rT[:, :, :, :], Pb[:, k - 1], SB[:, k], ALU.add, [tP, tSB], ["rT"])
                    cmul(Pb[:, k], tP, rT[:, :, :, :], ["rT"], A1, A2, QB)
            P8 = Pb[:, 7].rearrange("p r g (q b) -> p r g q b", q=nseq)
            if NB <= 4:
                for st in range(NB):
                    hin = HS[:, :, :, :, st]
                    tq = rT[:, :, :, 0:nseq]
                    cmul(tq, "rT", hin, ["HS"], AL[0][0], AL[0][1], nseq)
                    s.tt(e, HS[:, :, :, :, st + 1], tq, P8[:, :, :, :, st], ALU.add, ["rT", tP], ["HS"])
            else:
                ncol = NB + 1
                for q in range(nseq):
                    s.cp(e, HS[:, :, :, q, 1:ncol], P8[:, :, :, q, :], [tP], ["HS"])
                lv = 0
                while (1 << lv) < ncol:
                    sh = 1 << lv
                    nn = ncol - sh
                    for q in range(nseq):
                        tq = rT[:, :, :, 0:nn]
                        cmul(tq, "rT", HS[:, :, :, q, 0:nn], ["HS"], AL[lv][0], AL[lv][1], nn)
                        s.tt(e, HS[:, :, :, q, sh:ncol], HS[:, :, :, q, sh:ncol], tq, ALU.add, ["rT", "HS"], ["HS"])
                    lv += 1
            for q in range(nseq):
                s.cp(e, Qb[:, :, :, q * NB:(q + 1) * NB], HS[:, :, :, q, 0:NB], ["HS"], ["Q"])
            s.cp(e, Pb[:, 7], Qb[:, :, :, :], ["Q", "HS"], [tP])
            for k in range(1, 8):
                qa = Qb[:, :, :, :]
                cmul(qa, "Q", qa, ["Q"], A1, A2, QB)
                s.tt(e, Pb[:, k - 1], qa, Pb[:, k - 1], ALU.add, ["Q", tP], [tP])
            for ri in range(2):
                for q in range(nseq):
                    dst = HBh[0:64, ri, :, q, :].rearrange("p g (b i) -> p g b i", i=8)
                    pv = Pb[0:64, :, ri, :, q * NB:(q + 1) * NB]
                    pa = pv.ap
                    s.cp("act", dst[:, :, :, 0], pv[:, 7, :, :], [tP], [tHB])
                    src = AP(pv.tensor, pv.offset, [list(pa[0]), list(pa[2]), list(pa[3]), [pa[1][0], 7]])
                    s.cp("act", dst[:, :, :, 1:8], src, [tP], [tHB])
                    hv = HBh[64:128, ri, :, q, :]
                    ha = hv.ap
                    pv = Pb[64:128, :, ri, :, q * NB:(q + 1) * NB]
                    pa = pv.ap
                    d0 = AP(hv.tensor, hv.offset + (Cl - 1) * ha[2][0], [list(ha[0]), list(ha[1]), [-8 * ha[2][0], NB]])
                    s.cp("act", d0, pv[:, 7, :, :], [tP], [tHB])
                    d1 = AP(hv.tensor, hv.offset + (Cl - 2) * ha[2][0], [list(ha[0]), list(ha[1]), [-8 * ha[2][0], NB], [-ha[2][0], 7]])
                    src = AP(pv.tensor, pv.offset, [list(pa[0]), list(pa[2]), list(pa[3]), [pa[1][0], 7]])
                    s.cp("act", d1, src, [tP], [tHB])
            if ps == "A":
                src = HS[:, :, :, :, NB]
                srcp = AP(src.tensor, src.offset, [list(src.ap[0]), list(src.ap[3]), list(src.ap[1]), list(src.ap[2])])
                s.cp("act", NS[:, :, :, gsl], srcp, ["HS"], ["NS"])

        def st_out(gb):
            gsl, par, W1, W2, W4, tW1, tW2, tW4, SB, Pb, HBh, tSB, tP, tHB = ctx(gb)
            s.mark(ps + "s5_b%d_t2" % gb)
            for g0 in range(0, 8, GPB):
                pb = s.bank(s.B[4:8])
                pt = "B%d" % s.B.index(pb)
                for gi in range(GPB):
                    g = g0 + gi
                    for q in range(nseq):
                        o_ = pb[:, gi * CH + q * Cl:gi * CH + (q + 1) * Cl]
                        us = Ush[:, gb * 8 + g, q * Cl:(q + 1) * Cl]
                        s.mm(o_, W1[:, 0, g, :], us, True, False, [tW1, "UY"], [pt])
                        s.mm(o_, W1[:, 1, g, :], us, False, False, [tW1, "UY"], [pt])
                        s.mm(o_, W4[:, 0, g, :], HBh[:, 0, g, q, :], False, False, [tW4, tHB], [pt])
                        s.mm(o_, W4[:, 1, g, :], HBh[:, 1, g, q, :], False, True, [tW4, tHB], [pt])
                s.cp("act", Ysh[:, g0:g0 + GPB, :], pb[:, :].rearrange("p (g c) -> p g c", g=GPB), [pt], ["Ysh"])
            s.mark(ps + "s5_b%d_t14" % gb)
            for gh in range(2):
                pb = s.bank(s.B[4:8])
                pt = "B%d" % s.B.index(pb)
                for gi in range(4):
                    g = gh * 4 + gi
                    s.mm(pb[0:CH, gi * 128:(gi + 1) * 128], Ysh[:, g, :], ib[:, :], True, True, ["Ysh", "identb"], [pt])
                G0 = gb * 8 + gh * 4
                dstv = XY[0:CH, :, G0 * 16:(G0 + 4) * 16]
                dst4 = AP(dstv.tensor, dstv.offset, [list(dstv.ap[0]), [16, 4], list(dstv.ap[1]), [1, 16]])
                s.cp("act", dst4, pb[0:CH, :].rearrange("p (g i h) -> p g i h", g=4, i=8), [pt], ["XY"])


        e = "dve"
        wgen(0, 0)
        st_t2(0)
        for gb in range(8):
            if gb + 1 < 8:
                wgen(gb + 1, (gb + 1) % 2)
                st_t2(gb + 1)
            st_rec(gb)
            st_out(gb)

        if ps == "A":
            s.s5w_cached = True
            for d_ in range(2):
                s.dma("sp", s.d_ns[:, :, d_, :, :].rearrange("p q r g -> p q (r g)"),
                      NS[d_ * 64:(d_ + 1) * 64, :, :, :].rearrange("p q r g -> p q (r g)"), "nsout" if d_ == 0 else "nsout1", ["NS"], [])

        s.mark(ps + "s5_batches")
        IPB = 512 // CH
        for k in range(8):
            for i0 in range(0, 8, IPB):
                pb = s.bank(s.B[0:4])
                pt = "B%d" % s.B.index(pb)
                for ii in range(IPB):
                    i = i0 + ii
                    s.mm(pb[:, ii * CH:(ii + 1) * CH], XY[0:CH, i, k * 128:(k + 1) * 128], ib[0:CH, 0:CH], True, True,
                         ["XY", "identb"], [pt])
                dv = UY[:, k, 0:NT].rearrange("p (c i) -> p i c", i=8)[:, i0:i0 + IPB, :]
                s.cp("dve" if (k % 2 == 0) else "act", dv, pb[:, :].rearrange("p (i c) -> p i c", i=IPB), [pt], ["UY"])

        if s.stop_after == ps + "s5y":
            for k in range(8):
                s.cp("dve", s.xT[:, k, 0:NT], UY[:, k, :], ["UY"], ["xT"])
            return
        s.barrier()
        off[0] = off_glu
        gt = [[C([NT]) for _ in range(3)] for _ in range(2)]
        wg = [C([8, 2, 128], BF16) for _ in range(2)]
        wgs = [C([8, 2, 128]) for _ in range(2)]
        gs = [C([512]) for _ in range(4)]
        g1 = s.mod[:, s.ci, 1, 2, :]
        gsrc = s.d_glu.rearrange("(k p) (h n) -> p k h n", p=128, h=2)

        def loadg(m):
            ss = m % 2
            for h_ in range(2):
                s.dma("sp", wgs[ss][:, :, h_, :], gsrc[:, :, h_, m * 128:(m + 1) * 128], "glus%dh%d" % (ss, h_), [],
                      ["glus%dh%d" % (ss, h_)])

        loadg(0)
        loadg(1)
        for k in range(8):
            b_ = k % 2
            t, x2, q = gt[b_][0][:, :], gt[b_][1][:, :], gt[b_][2][:, :]
            tk = ["gt%d_%d" % (b_, i) for i in range(3)]
            s.stt("dve", t, uT[:, k, 0:NT], s.spv("s5D", k), UY[:, k, :], ALU.mult, ALU.add, ["hT", "UY", "smallp"], [tk[0]])
            s.act(x2, t, AF.Square, [tk[0]], [tk[1]])
            s.ts("dve", x2, x2, 0.044715, 1.0, ALU.mult, ALU.add, [tk[1]], [tk[1]])
            s.tt("pool", q, x2, t, ALU.mult, [tk[1], tk[0]], [tk[2]])
            s.act(q, q, AF.Sigmoid, [tk[2]], [tk[2]], scale=2.0 * 0.7978845608028654)
            s.tt("dve", UY[:, k, :], q, t, ALU.mult, [tk[2], tk[0]], ["UY"])
        s.mark(ps + "s5_gelu")
        for m in range(8):
            slot = m % 2
            tok = "wg%d" % slot
            s.cp("act", wg[slot][:], wgs[slot][:], ["glus%dh0" % slot, "glus%dh1" % slot], [tok])
            if m + 2 < 8:
                loadg(m + 2)
            for tb in range(NTB):
                tsl = slice(tb * 512, (tb + 1) * 512)
                ba = s.bank(s.B[0:6])
                bb = s.bank(s.B[0:6])
                ta, tbk = "B%d" % s.B.index(ba), "B%d" % s.B.index(bb)
                for k in range(8):
                    s.mm(ba[:, :], wg[slot][:, k, 0, :], UY[:, k, tsl], k == 0, k == 7, [tok, "UY"], [ta])
                for k in range(8):
                    s.mm(bb[:, :], wg[slot][:, k, 1, :], UY[:, k, tsl], k == 0, k == 7, [tok, "UY"], [tbk])
                ix = (m * NTB + tb) % 2
                sg, tm = gs[ix][:, :], gs[2 + ix][:, :]
                s.act(sg, bb[:, :], AF.Sigmoid, [tbk, "smallp"], ["gsg%d" % ix], bias=s.spv("glub", 8 + m), scale=1.0)
                s.stt("dve", tm, ba[:, :], s.spv("glub", m), sg, ALU.add, ALU.mult, [ta, "gsg%d" % ix, "smallp"], ["gtm%d" % ix])
                s.stt("dve", s.xT[:, m, tsl], tm, g1[:, m:m + 1], s.xT[:, m, tsl], ALU.mult, ALU.add,
                      ["gtm%d" % ix, "mod", "xT"], ["xT"])


_CACHE = {}


def _feat_pk(v):
    v = np.asarray(v, np.float32)
    return np.ascontiguousarray(v.reshape(-1, 128).T)


def _host_inputs(inp):
    f32 = np.float32
    g = {}
    sp = np.zeros((128, NSP), f32)

    def put(name, arr):
        a = _feat_pk(arr)
        sp[:, SP_OFF[name]:SP_OFF[name] + a.shape[1]] = a

    put("n1g", inp["norm1_g"].reshape(-1))
    put("n2g", inp["norm2_g"].reshape(-1))
    put("fing", inp["final_g"].reshape(-1))
    put("adab", inp["ada_b"].reshape(-1))
    put("hyinb", inp["hy_in_b"].reshape(-1))
    put("convw", inp["hy_conv_w"].reshape(-1))
    put("convb", inp["hy_conv_b"].reshape(-1))
    put("fbias", inp["hy_fbias"].reshape(-1))
    put("hyoutb", inp["hy_out_b"].reshape(-1))
    put("s5D", inp["s5_D"].reshape(-1))
    put("glub", inp["s5_glu_b"].reshape(-1))
    g["smallp"] = sp
    g["ident"] = np.eye(128, dtype=f32)
    g["posT"] = _pos_embed_T()
    g["ada_w"] = np.ascontiguousarray(inp["ada_w"], f32)
    g["ffn_w13"] = np.ascontiguousarray(inp["ffn_w13"], f32)
    g["ffn_w2"] = np.ascontiguousarray(inp["ffn_w2"], f32)
    g["hy_in_w"] = np.ascontiguousarray(inp["hy_in_w"][0], f32)
    g["hy_out_w"] = np.ascontiguousarray(inp["hy_out_w"][0], f32)
    g["s5_glu_w"] = np.ascontiguousarray(inp["s5_glu_w"][0], f32)
    g["pe_w1"] = np.ascontiguousarray(inp["hy_pe_w1"][0], f32)
    g["pe_w2"] = np.ascontiguousarray(inp["hy_pe_w2"][0], f32)
    g["pe_w3"] = np.ascontiguousarray(inp["hy_pe_w3"][0], f32)
    g["pe_small"] = np.ascontiguousarray(
        np.stack([inp["hy_pe_b1"][0], inp["hy_pe_b2"][0], inp["hy_freq"][0]], axis=1), f32)
    for nm, L in (("A", 256), ("B", 1024)):
        c4, s4, phi = _dft_consts(L)
        zT, decf, decb = _hyena_consts(L)
        g["c4_" + nm] = c4.astype(ml_dtypes.bfloat16)
        g["s4_" + nm] = s4.astype(ml_dtypes.bfloat16)
        g["phi_" + nm] = phi
        g["zT_" + nm] = zT
        g["decf_" + nm] = decf
        g["decb_" + nm] = decb
    are = np.transpose(inp["s5_A_re"][0], (2, 0, 1)).reshape(64, 128)
    aim = np.transpose(inp["s5_A_im"][0], (2, 0, 1)).reshape(64, 128)
    ldt = np.broadcast_to(inp["s5_log_dt"][0].reshape(1, 128), (64, 128))
    g["s5_a"] = np.ascontiguousarray(np.stack([are, aim, ldt], axis=1), f32)
    ea = np.zeros((2, 64, 8), f32)
    ef = np.zeros((2, 64, 8), f32)
    for j in range(8):
        ea[0, :, j] = -1 - j
        ea[1, :, j] = j - 8
        ef[0, :, j] = j + 1
        ef[1, :, j] = 8 - j
    e = np.stack([ea[:, 0, :], ef[:, 0, :]], axis=0).reshape(-1)
    g["s5_e"] = np.ascontiguousarray(np.broadcast_to(e[None], (64, 32)), f32)
    Bt = np.stack([np.transpose(inp["s5_B_re"][0], (2, 0, 1, 3)), np.transpose(inp["s5_B_im"][0], (2, 0, 1, 3))], axis=1)
    Ct = np.stack([np.transpose(inp["s5_C_re"][0], (3, 0, 1, 2)), np.transpose(inp["s5_C_im"][0], (3, 0, 1, 2))], axis=1)
    g["s5_Bt"] = np.ascontiguousarray(Bt, f32)
    g["s5_Ct"] = np.ascontiguousarray(Ct, f32)
    mask = np.zeros((128, 2, 128), f32)
    for j in range(8):
        for i in range(8):
            if j <= i:
                mask[j * 16:(j + 1) * 16, 0, i * 16:(i + 1) * 16] = 1.0
            if j >= i:
                mask[j * 16:(j + 1) * 16, 1, i * 16:(i + 1) * 16] = 1.0
    g["w1mask"] = mask
    return g


def kernel(**inp):
    stop_after = os.environ.get("KSTOP") or None
    inp = {k: np.asarray(v) for k, v in inp.items()}
    key = stop_after
    if key not in _CACHE:
        _CACHE[key] = Builder(stop_after).build()
    nc = _CACHE[key]
    g = _host_inputs(inp)
    xp = inp["x_prompt"].astype(np.float32)
    xs = inp["x_sample"].astype(np.float32)
    in_maps = []
    for c in range(8):
        m = dict(g)
        m["xT_A"] = np.ascontiguousarray(xp[2 * c:2 * c + 2].reshape(512, D).T)
        m["xT_B"] = np.ascontiguousarray(xs[c].T)
        cond = np.stack([inp["c_ctx"].astype(np.float32), inp["c"][c].astype(np.float32)], axis=1)
        m["condT"] = np.ascontiguousarray(cond.reshape(8, 128, 2).transpose(1, 0, 2))
        m["s5_st0"] = np.ascontiguousarray(np.transpose(inp["state_s5"][c, 0], (3, 0, 1, 2)), np.float32)
        in_maps.append(m)
    res = run_bass_kernel_spmd(nc, in_maps, core_ids=list(range(8)))
    R = res.results
    if stop_after is not None:
        return [np.asarray(r["dbg"]) for r in R]
    y_prompt = np.zeros((16, 256, D), np.float32)
    y_sample = np.zeros((8, 1024, D), np.float32)
    new_state = np.zeros((16, 1, 2, 2, 64, 64), np.float32)
    for c in range(8):
        y_prompt[2 * c:2 * c + 2] = np.asarray(R[c]["yT_A"]).T.reshape(2, 256, D)
        y_sample[c] = np.asarray(R[c]["yT_B"]).T
        ns = np.asarray(R[c]["ns_out"])
        new_state[2 * c:2 * c + 2, 0] = np.transpose(ns, (1, 2, 3, 4, 0))
    return (y_prompt, y_sample, new_state)
```

```python
import math
import os
import contextlib
import numpy as np
import ml_dtypes
import concourse.bass as bass
import concourse.mybir as mybir
from concourse.bass_utils import run_bass_kernel_spmd
from concourse.ap import AP

F32 = mybir.dt.float32
BF16 = mybir.dt.bfloat16
I32 = mybir.dt.int32
AF = mybir.ActivationFunctionType
ALU = mybir.AluOpType

D = 1024
DFF = 2816
NKF = 22
EPS = 1e-6
TWO_PI = 2.0 * math.pi


class Prog:
    def __init__(self, nc):
        self.nc = nc
        self.ops = []
        self.last_w = {}
        self.readers = {}
        self.dreaders = {}
        self.phase_tok = "__phase__"
        self.engs = {"pe": nc.tensor, "dve": nc.vector, "act": nc.scalar,
                     "pool": nc.gpsimd, "sp": nc.sync}

    def add(self, eng, fn, reads=(), writes=(), dma=None, small=False):
        idx = len(self.ops)
        deps = set()
        reads = tuple(reads) + (self.phase_tok,)
        for t in reads:
            w = self.last_w.get(t)
            if w is not None:
                deps.add(w)
        for t in writes:
            w = self.last_w.get(t)
            if w is not None:
                deps.add(w)
            r = self.readers.get(t)
            if r:
                deps.update(r.values())
            r = self.dreaders.get(t)
            if r:
                deps.update(r)
        need = set()
        for d in deps:
            o = self.ops[d]
            if o[3] is not None or o[0] != eng or (o[6] and eng != "pe"):
                need.add(d)
        self.ops.append([eng, fn, need, dma, False, 0, small])
        for t in reads:
            if dma is not None:
                self.dreaders.setdefault(t, []).append(idx)
            else:
                self.readers.setdefault(t, {})[eng] = idx
        for t in writes:
            self.last_w[t] = idx
            self.readers[t] = {}
            self.dreaders[t] = []
        return idx

    def emit(self, sems):
        ops = self.ops
        for o in ops:
            for d in o[2]:
                ops[d][4] = True
        cnt = {}
        for o in ops:
            if o[3] is not None:
                k = o[3]
                cnt[k] = cnt.get(k, 0) + 16
                o[5] = cnt[k]
            elif o[4]:
                k = o[0]
                cnt[k] = cnt.get(k, 0) + 1
                o[5] = cnt[k]
        seen = {e: {} for e in self.engs}
        for o in ops:
            eng, fn, need, dma, sig, val = o[:6]
            E = self.engs[eng]
            waits = {}
            for d in need:
                od = ops[d]
                k = od[3] if od[3] is not None else od[0]
                if od[5] > waits.get(k, 0):
                    waits[k] = od[5]
            for k, v in waits.items():
                if seen[eng].get(k, 0) < v:
                    E.wait_ge(sems[k], v)
                    seen[eng][k] = v
            ins = fn(E)
            if dma is not None:
                ins.then_inc(sems[dma], 16)
            elif sig:
                ins.then_inc(sems[eng], 1)
        return cnt


def _dma_keys():
    keys = ["init%d" % i for i in range(24)]
    keys += ["x_in", "pos", "ada0", "ada1", "ada2", "hyin0", "hyin1", "hyin2", "hyout0", "hyout1",
             "w13_0", "w13_1", "w13_2", "w2_0", "w2_1", "glu0", "glu1", "decf", "decb", "w3cb",
             "s5b", "s5c", "s5st", "yout", "nsout", "dbg", "pos0", "pos1", "s5t0", "s5t1", "s5A", "scrw0", "scrw1",
             "w13s0h0", "w13s0h1", "w13s1h0", "w13s1h1", "w2s0", "w2s1", "glus0h0", "glus0h1", "glus1h0", "glus1h1",
             "cw1_0", "cw1_1", "cw2_0", "cw2_1", "cw4_0", "cw4_1", "s5A1", "s5st1", "s5b1", "s5c1", "nsout1"]
    return keys


def _lay_pk(a):
    K = a.shape[0] // 128
    return np.ascontiguousarray(a.reshape(K, 128, *a.shape[1:]).swapaxes(0, 1))


def _dft_consts(L):
    t = np.arange(L, dtype=np.float64)[:, None]
    f = np.arange(L, dtype=np.float64)[None, :]
    th = np.pi * (f + 0.5) * (t + 0.5) / L
    C4 = np.cos(th).astype(np.float32)
    S4 = (-np.sin(th)).astype(np.float32)
    phi = np.pi * (np.arange(L, dtype=np.float64) + 0.5) / (2 * L)
    cph = _lay_pk(np.cos(phi).astype(np.float32)[:, None])[:, :, 0]
    sph = _lay_pk(np.sin(phi).astype(np.float32)[:, None])[:, :, 0]
    return _lay_pk(C4), _lay_pk(S4), np.ascontiguousarray(np.concatenate([cph, sph, -sph], axis=1))


def _hyena_consts(L):
    f32 = np.float32
    t = np.linspace(0.0, 1.0, L, dtype=f32)[:, None]
    w = (2.0 * math.pi * np.arange(L, dtype=f32)[:, None] / L).astype(f32)
    bands = np.linspace(1e-4, 15, 16, dtype=f32)[None, :]
    z = np.concatenate([t, np.cos(bands * w), -np.sin(bands * w)], axis=-1).astype(f32)
    zT = np.zeros((33, L + 1), f32)
    zT[:, :L] = z.T
    max_decay = math.log(1e-2) / 0.3
    min_decay = math.log(1e-2) / 1.5
    deltas = np.abs(np.linspace(min_decay, max_decay, D, dtype=f32))
    decay = np.exp(-t * deltas[None, :]).astype(f32)
    decb = np.zeros_like(decay)
    decb[:L - 1] = decay[1:]
    return zT, _lay_pk(decay), _lay_pk(decb)


def _pos_embed_T():
    f32 = np.float32
    rows, GW = 16, 64
    quarter = D // 4
    omega = (1.0 / (10000.0 ** (np.arange(quarter, dtype=f32) / quarter))).astype(f32)

    def axis_embed(n):
        ang = np.arange(n, dtype=f32)[:, None] * omega[None]
        return np.concatenate([np.sin(ang), np.cos(ang)], axis=-1).astype(f32)

    er = np.broadcast_to(axis_embed(rows)[:, None], (rows, GW, D // 2))
    ec = np.broadcast_to(axis_embed(GW)[None], (rows, GW, D // 2))
    pe = np.concatenate([er, ec], axis=-1).reshape(rows * GW, D)
    return np.ascontiguousarray(pe.T)


SP_OFF = {}


def _sp_layout():
    off = 0
    for name, n in [("n1g", 16), ("n2g", 16), ("fing", 8), ("adab", 96), ("hyinb", 24), ("convw", 72),
                    ("convb", 24), ("fbias", 16), ("hyoutb", 8), ("s5D", 8), ("glub", 16)]:
        SP_OFF[name] = off
        off += n
    return off


NSP = _sp_layout()


class Builder:
    def __init__(self, stop_after=None):
        self.stop_after = stop_after
        self.nc = bass.Bass("TRN2", target_bir_lowering=False)
        self.P = Prog(self.nc)
        self.st = contextlib.ExitStack()
        self.ninit = 0
        self.ps_rr = 0
        self.marks = []

    def din(self, name, shape, dt=F32):
        return self.nc.dram_tensor(name, list(shape), dt, kind="ExternalInput").ap()

    def dout(self, name, shape, dt=F32):
        return self.nc.dram_tensor(name, list(shape), dt, kind="ExternalOutput").ap()

    def sb(self, name, shape, dt=F32):
        return self.st.enter_context(self.nc.sbuf_tensor("sb_" + name, list(shape), dt))

    def mm(self, out, lhsT, rhs, start, stop, reads, writes):
        self.P.add("pe", lambda E: E.matmul(out, lhsT=lhsT, rhs=rhs, start=start, stop=stop), reads, writes)

    @staticmethod
    def _small(ap):
        n = 1
        for d in ap.shape[1:]:
            n *= d
        return n < 400

    def act(self, out, in_, func, reads, writes, bias=None, scale=None):
        kw = {}
        if bias is not None:
            kw["bias"] = bias
        if scale is not None:
            kw["scale"] = scale
        self.P.add("act", lambda E: E.activation(out=out, in_=in_, func=func, **kw), reads, writes, small=self._small(out))

    def tt(self, eng, out, in0, in1, op, reads, writes):
        self.P.add(eng, lambda E: E.tensor_tensor(out=out, in0=in0, in1=in1, op=op), reads, writes, small=self._small(out))

    def ts(self, eng, out, in0, s1, s2, op0, op1, reads, writes):
        if s2 is None:
            self.P.add(eng, lambda E: E.tensor_single_scalar(out=out, in_=in0, scalar=s1, op=op0), reads, writes,
                       small=self._small(out))
        else:
            self.P.add(eng, lambda E: E.tensor_scalar(out=out, in0=in0, scalar1=s1, scalar2=s2, op0=op0, op1=op1),
                       reads, writes, small=self._small(out))

    def stt(self, eng, out, in0, scalar, in1, op0, op1, reads, writes):
        self.P.add(eng, lambda E: E.scalar_tensor_tensor(out=out, in0=in0, scalar=scalar, in1=in1, op0=op0, op1=op1),
                   reads, writes, small=self._small(out))

    def cp(self, eng, out, in_, reads, writes):
        if eng == "act":
            self.P.add(eng, lambda E: E.copy(out=out, in_=in_), reads, writes, small=self._small(out))
        else:
            self.P.add(eng, lambda E: E.tensor_copy(out=out, in_=in_), reads, writes, small=self._small(out))

    def memset(self, eng, ap, val, writes):
        self.P.add(eng, lambda E: E.memset(ap, val), (), writes, small=self._small(ap))

    def recip(self, out, in_, reads, writes):
        self.P.add("dve", lambda E: E.reciprocal(out=out, in_=in_), reads, writes, small=self._small(out))

    def dma(self, q, out, in_, key, reads, writes):
        self.P.add(q, lambda E: E.dma_start(out=out, in_=in_), reads, writes, dma=key)

    def init_load(self, out, in_, token, cast=False):
        key = "init%d" % self.ninit
        self.ninit += 1
        self.dma("pool" if cast else "sp", out, in_, key, (), [token])

    def barrier(self):
        d = self.dummy
        self.P.add("dve", lambda E: E.memset(d[:, 0:1], 0.0), (), [self.P.phase_tok])

    def bank(self, lst):
        b = lst[self.ps_rr % len(lst)]
        self.ps_rr += 1
        return b

    def build(self):
        nc = self.nc
        with self.st:
            self._declare()
            self._init_loads()
            self.s5_prep()
            if self.stop_after is None or not self.stop_after.startswith("B"):
                self.run_pass("A")
            if self.stop_after is None or not self.stop_after.startswith("A"):
                self.run_pass("B")
            sems = {}
            for k in ["pe", "dve", "act", "pool", "sp"] + _dma_keys():
                sems[k] = self.st.enter_context(nc.semaphore(k))
            cnt = self.P.emit(sems)
            for k in ["yout", "nsout", "nsout1", "dbg"]:
                if k in cnt:
                    nc.sync.wait_ge(sems[k], cnt[k])
        return nc

    def _declare(self):
        s = self
        s.d_xA = s.din("xT_A", [D, 512])
        s.d_xB = s.din("xT_B", [D, 1024])
        s.d_pos = s.din("posT", [D, 1024])
        s.d_cond = s.din("condT", [128, 8, 2])
        s.d_smallp = s.din("smallp", [128, NSP])
        s.d_ident = s.din("ident", [128, 128])
        s.d_ada_w = s.din("ada_w", [2, D, 6 * D])
        s.d_w13 = s.din("ffn_w13", [2, D, 2 * DFF])
        s.d_w2 = s.din("ffn_w2", [2, DFF, D])
        s.d_hyin = s.din("hy_in_w", [D, 3 * D])
        s.d_hyout = s.din("hy_out_w", [D, D])
        s.d_glu = s.din("s5_glu_w", [D, 2 * D])
        s.d_pe_w1 = s.din("pe_w1", [33, 64])
        s.d_pe_w2 = s.din("pe_w2", [64, 64])
        s.d_pe_w3 = s.din("pe_w3", [64, 4096])
        s.d_pe_small = s.din("pe_small", [64, 3])
        s.d_c4 = {"A": s.din("c4_A", [128, 2, 256], BF16), "B": s.din("c4_B", [128, 8, 1024], BF16)}
        s.d_s4 = {"A": s.din("s4_A", [128, 2, 256], BF16), "B": s.din("s4_B", [128, 8, 1024], BF16)}
        s.d_phi = {"A": s.din("phi_A", [128, 6]), "B": s.din("phi_B", [128, 24])}
        s.d_zT = {"A": s.din("zT_A", [33, 257]), "B": s.din("zT_B", [33, 1025])}
        s.d_decf = {"A": s.din("decf_A", [128, 2, D]), "B": s.din("decf_B", [128, 8, D])}
        s.d_decb = {"A": s.din("decb_A", [128, 2, D]), "B": s.din("decb_B", [128, 8, D])}
        s.d_s5a = s.din("s5_a", [64, 3, 128])
        s.d_s5e = s.din("s5_e", [64, 32])
        s.d_s5B = s.din("s5_Bt", [64, 2, 2, 64, 16])
        s.d_s5C = s.din("s5_Ct", [64, 2, 2, 64, 16])
        s.d_s5st = s.din("s5_st0", [64, 2, 2, 64])
        s.d_mask = s.din("w1mask", [128, 2, 128])
        s.d_s5tab = s.nc.dram_tensor("s5tab_scr", [64, 5, 1024], F32).ap()
        s.d_s5A = s.nc.dram_tensor("s5A_scr", [64, 12, 256], F32).ap()
        s.d_cw1 = s.nc.dram_tensor("cw1_scr", [8, 128, 2048], BF16).ap()
        s.d_cw2 = s.nc.dram_tensor("cw2_scr", [8, 128, 2048], BF16).ap()
        s.d_cw4 = s.nc.dram_tensor("cw4_scr", [8, 128, 2048], BF16).ap()
        s.d_yA = s.dout("yT_A", [D, 512])
        s.d_yB = s.dout("yT_B", [D, 1024])
        s.d_ns = s.dout("ns_out", [64, 2, 2, 2, 64])
        if s.stop_after is not None:
            s.d_dbg = s.dout("dbg", [D, 1024])
        s.B = [s.st.enter_context(s.nc.psum_tensor("bank%d" % i, [128, 512], F32)) for i in range(8)]
        s.smallp = s.sb("smallp", [128, NSP])
        s.ident = s.sb("ident", [128, 128])
        s.identb = s.sb("identb", [128, 128], BF16)
        s.onesf = s.sb("onesf", [128, 128])
        s.onesb = s.sb("onesb", [128, 128], BF16)
        s.xT = s.sb("xT", [128, 8, 1024])
        s.hT = s.sb("hT", [128, 8, 1024], BF16)
        s.mod = s.sb("mod", [128, 2, 2, 6, 8])
        s.modA = s.sb("modA", [128, 2, 2, 2, 8])
        s.gb = s.sb("gb", [128, 2, 8])
        s.cond = s.sb("cond", [128, 8, 2])
        s.condb = s.sb("condb", [128, 8, 2], BF16)
        s.dummy = s.sb("dummy", [128, 4])
        s.nscr = s.sb("nscr", [128, 12288 // 4])
        s.ARENA = 143 * 1024
        s.arena = s.sb("arena", [128, s.ARENA // 4])

    def carve(self, off_bytes, shape, dt=F32, parts=128):
        esz = 4 if dt in (F32, I32) else 2
        n = 1
        for d in shape:
            n *= d
        assert off_bytes % 4 == 0
        assert off_bytes + n * esz <= self.ARENA, (off_bytes, shape)
        base = self.arena[0:parts, off_bytes // 4: off_bytes // 4 + (n * esz + 3) // 4]
        if dt != F32:
            base = base.bitcast(dt)
            base = base[:, 0:n]
        if len(shape) == 1:
            return base
        names = " ".join("d%d" % i for i in range(len(shape)))
        kw = {"d%d" % i: shape[i] for i in range(1, len(shape))}
        return base.rearrange("p (%s) -> p %s" % (names, names), **kw)

    def _init_loads(self):
        s = self
        s.init_load(s.smallp[:], s.d_smallp[:, :], "smallp")
        s.init_load(s.ident[:], s.d_ident[:, :], "ident")
        s.init_load(s.identb[:], s.d_ident[:, :], "identb", cast=True)
        s.init_load(s.cond[:], s.d_cond[:, :, :], "cond")
        s.memset("dve", s.onesf[:], 1.0, ["onesf"])
        s.memset("dve", s.onesb[:], 1.0, ["onesb"])

    def spv(self, name, idx, n=1):
        o = SP_OFF[name] + idx
        return self.smallp[:, o:o + n]

    def run_pass(self, ps):
        s = self
        s.ps = ps
        s.NT = 512 if ps == "A" else 1024
        s.L = 256 if ps == "A" else 1024
        s.KT = s.L // 128
        s.seqs = [(0, 256), (256, 512)] if ps == "A" else [(0, 1024)]
        s.ci = 0 if ps == "A" else 1
        s.NTB = s.NT // 512
        s.load_x()
        s.ada()
        if s.stop(ps + "ada"):
            return
        s.hyena_layer()
        if s.stop(ps + "hy"):
            return
        s.ffn(0)
        if s.stop(ps + "ffn0"):
            return
        s.s5_layer()
        if s.stop(ps + "s5y") or s.stop(ps + "s5h"):
            return
        if s.stop(ps + "s5"):
            return
        s.ffn(1)
        if s.stop(ps + "ffn1"):
            return
        s.final_norm()
        s.mark(ps + "final")

    def mark(self, name):
        n = sum(1 for o in self.P.ops if o[0] == "pe")
        self.marks.append((name, n))

    def stop(self, name):
        s = self
        s.mark(name)
        if s.stop_after == name:
            src = s.xT[:, :, 0:s.NT]
            dst = s.d_dbg.rearrange("(k p) t -> p k t", p=128)[:, :, 0:s.NT]
            s.dma("sp", dst, src, "dbg", ["xT"], [])
            return True
        return False

    def load_x(self):
        s = self
        s.barrier()
        NT = s.NT
        src = (s.d_xA if s.ps == "A" else s.d_xB).rearrange("(k p) t -> p k t", p=128)
        s.dma("sp", s.xT[:, :, 0:NT], src, "x_in", [], ["xT"])
        if s.ps == "B":
            psrc = s.d_pos.rearrange("(k p) t -> p k t", p=128)
            xv = s.xT[:, :, 0:NT]
            s.P.add("pool", lambda E: E.dma_start(out=xv, in_=psrc, accum_op=ALU.add), ["xT"], ["xT"], dma="pos0")

    def ada(self):
        s = self
        if getattr(s, "ada_done", False):
            return
        s.ada_done = True
        s.barrier()
        s.act(s.condb[:], s.cond[:], AF.Silu, ["cond"], ["condb"])
        adaw = [s.carve(16384 * i, [8, 1024], BF16) for i in range(3)]
        n = 0
        for i in range(2):
            wsrc = s.d_ada_w[i].rearrange("(k p) n -> p k n", p=128)
            for j in range(6):
                slot = n % 3
                n += 1
                tok = "adaw%d" % slot
                s.dma("pool", adaw[slot][:], wsrc[:, :, j * 1024:(j + 1) * 1024], "ada%d" % slot, [], [tok])
                pb = s.B[7]
                for m in range(8):
                    for k in range(8):
                        s.mm(pb[:, 2 * m:2 * m + 2], adaw[slot][:, k, m * 128:(m + 1) * 128], s.condb[:, k, 0:2],
                             k == 0, k == 7, [tok, "condb"], ["B7"])
                pv = pb[:, 0:16].rearrange("p (m c) -> p c m", c=2)
                for ci in range(2):
                    s.tt("dve", s.mod[:, ci, i, j, :], pv[:, ci, :], s.spv("adab", (i * 6 + j) * 8, 8), ALU.add,
                         ["B7", "smallp"], ["mod"])
        for ci in range(2):
            for i in range(2):
                for w, (gname, jsc) in enumerate([("n1g", 1), ("n2g", 4)]):
                    s.stt("dve", s.modA[:, ci, i, w, :], s.mod[:, ci, i, jsc, :], 1.0, s.spv(gname, i * 8, 8), ALU.add, ALU.mult,
                          ["mod", "smallp"], ["modA"])
            s.tt("dve", s.gb[:, ci, :], s.mod[:, ci, 0, 2, :], s.spv("hyoutb", 0, 8), ALU.mult, ["mod", "smallp"], ["gb"])

    def norm_mod(self, A, Bsh, dst_bf16=None, dst_f32=None):
        s = self
        sq = s.nscr[:, 0:2048].bitcast(BF16).rearrange("p (k t) -> p k t", k=8)
        rstd = s.nscr[:, 2048:2560]
        tmp = s.nscr[:, 2560:3072].rearrange("p (a t) -> p a t", a=1)
        for tb in range(s.NTB):
            tsl = slice(tb * 512, (tb + 1) * 512)
            for k in range(8):
                s.act(sq[:, k, :], s.xT[:, k, tsl], AF.Square, ["xT"], ["nsq"])
            pb = s.B[7]
            for k in range(8):
                s.mm(pb[:, :], s.onesb[:], sq[:, k, :], k == 0, k == 7, ["onesb", "nsq"], ["B7"])
            s.act(rstd[:], pb[:, :], AF.Sqrt, ["B7"], ["nrstd"], bias=EPS, scale=1.0 / D)
            s.recip(rstd[:], rstd[:], ["nrstd"], ["nrstd"])
            for k in range(8):
                if dst_f32 is not None:
                    s.stt("dve", dst_f32[:, k, tsl], s.xT[:, k, tsl], A[:, k:k + 1], rstd[:], ALU.mult, ALU.mult,
                          ["xT", "nrstd", "modA", "smallp"], ["nout"])
                else:
                    t = tmp[:, 0, :]
                    s.stt("dve", t, s.xT[:, k, tsl], A[:, k:k + 1], rstd[:], ALU.mult, ALU.mult,
                          ["xT", "nrstd", "modA", "smallp"], ["ntmp0"])
                    s.act(dst_bf16[:, k, tsl], t, AF.Identity, ["ntmp0", "mod"], ["hT"],
                          bias=Bsh[:, k:k + 1], scale=1.0)

    def ffn(self, layer):
        s = self
        s.barrier()
        NT, NTB = s.NT, s.NTB
        s.norm_mod(s.modA[:, s.ci, layer, 1, :], s.mod[:, s.ci, layer, 3, :], dst_bf16=s.hT)
        g2 = s.mod[:, s.ci, layer, 5, :]
        sT = s.carve(0, [NKF, 1024], BF16)
        o_ = 45056
        w13 = [s.carve(o_ + 4096 * i, [8, 2, 128], BF16) for i in range(3)]
        o_ += 12288
        w13s = [s.carve(o_ + 8192 * i, [8, 2, 128]) for i in range(2)]
        o_ += 16384
        w2b = [s.carve(o_ + 5632 * i, [NKF, 128], BF16) for i in range(2)]
        o_ += 11264
        w2s = [s.carve(o_ + 11264 * i, [NKF, 128]) for i in range(2)]
        o_ += 22528
        stmp = s.carve(o_, [2, 512])
        wsrc = s.d_w13[layer].rearrange("(k p) (h n) -> p k h n", p=128, h=2)
        w2src = s.d_w2[layer].rearrange("(k p) n -> p k n", p=128)

        def load13(m):
            ss = m % 2
            for h_ in range(2):
                s.dma("sp", w13s[ss][:, :, h_, :], wsrc[:, :, h_, m * 128:(m + 1) * 128], "w13s%dh%d" % (ss, h_), [],
                      ["w13s%dh%d" % (ss, h_)])

        def load2(m):
            ss = m % 2
            s.dma("sp", w2s[ss][:], w2src[:, :, m * 128:(m + 1) * 128], "w2s%d" % ss, [], ["w2s%d" % ss])

        load13(0)
        load13(1)
        load2(0)
        load2(1)
        for m in range(NKF):
            slot = m % 3
            ss = m % 2
            tok = "w13_%d" % slot
            s.cp("act", w13[slot][:], w13s[ss][:], ["w13s%dh0" % ss, "w13s%dh1" % ss], [tok])
            if m + 2 < NKF:
                load13(m + 2)
            for tb in range(NTB):
                tsl = slice(tb * 512, (tb + 1) * 512)
                ba = s.bank(s.B[0:6])
                bb = s.bank(s.B[0:6])
                ta, tbk = "B%d" % s.B.index(ba), "B%d" % s.B.index(bb)
                for k in range(8):
                    s.mm(ba[:, :], w13[slot][:, k, 0, :], s.hT[:, k, tsl], k == 0, k == 7, [tok, "hT"], [ta])
                for k in range(8):
                    s.mm(bb[:, :], w13[slot][:, k, 1, :], s.hT[:, k, tsl], k == 0, k == 7, [tok, "hT"], [tbk])
                st_ = stmp[:, (m * NTB + tb) % 2, :]
                stok = "stmp%d" % ((m * NTB + tb) % 2)
                s.act(st_, ba[:, :], AF.Silu, [ta], [stok])
                s.tt("dve", sT[:, m, tsl], st_, bb[:, :], ALU.mult, [stok, tbk], ["sT"])
        for m in range(8):
            slot = m % 2
            tok = "w2_%d" % slot
            s.cp("act", w2b[slot][:], w2s[slot][:], ["w2s%d" % slot], [tok])
            if m + 2 < 8:
                load2(m + 2)
            for tb in range(NTB):
                tsl = slice(tb * 512, (tb + 1) * 512)
                pb = s.bank(s.B[0:6])
                pt = "B%d" % s.B.index(pb)
                for k in range(NKF):
                    s.mm(pb[:, :], w2b[slot][:, k, :], sT[:, k, tsl], k == 0, k == NKF - 1, [tok, "sT"], [pt])
                s.stt("dve", s.xT[:, m, tsl], pb[:, :], g2[:, m:m + 1], s.xT[:, m, tsl], ALU.mult, ALU.add,
                      [pt, "mod", "xT"], ["xT"])

    def final_norm(self):
        s = self
        s.barrier()
        yo = s.carve(32768, [8, 1024])
        s.norm_mod(s.spv("fing", 0, 8), None, dst_f32=yo)
        dst = (s.d_yA if s.ps == "A" else s.d_yB).rearrange("(k p) t -> p k t", p=128)
        s.dma("sp", dst, yo[:, :, 0:s.NT], "yout", ["nout"], [])

    def hyena_layer(self):
        s = self
        s.barrier()
        ps, NT, NTB, L, KT = s.ps, s.NT, s.NTB, s.L, s.KT
        off = [0]

        def C(shape, dt=F32, parts=128):
            esz = 4 if dt in (F32, I32) else 2
            n = 1
            for d in shape:
                n *= d
            v = s.carve(off[0], shape, dt, parts)
            off[0] += (n * esz + 3) // 4 * 4
            return v

        c4 = C([KT, L], BF16)
        s4 = C([KT, L], BF16)
        hyo = C([8, NT], BF16)
        phi = C([3 * KT])
        h2T = C([L + 1])
        wcb = [C([8, 3, 128], BF16)] * 2
        wo = [C([8, 128], BF16)] * 2
        zq = C([NT])
        vT = C([NT])
        x1T = C([NT])
        x2T = C([NT])
        v2T = C([NT])
        vb = C([NT], BF16)
        vtok = C([NT // 128, 128], BF16)
        Yb = C([2 * KT, 128], BF16)
        Kr = C([KT, 2, 128])
        Ki = C([KT, 2, 128])
        decf = C([KT, 128])
        decb = C([KT, 128])
        w3cb = C([2, 2, 128])
        hd = C([2, 2, 128])
        apb = C([KT, 2, 128], BF16)
        amb = C([KT, 2, 128], BF16)
        rn = C([256])
        arow = C([256])
        off_re = off[0]
        zT = C([L + 1])
        h1T = C([L + 1])
        mtmp = C([512])
        mtmpi = C([512], I32)
        pw1 = C([64])
        pw2 = C([64])
        psm = C([3])
        pq = C([4])
        off[0] = off_re
        cm = [C([512]) for _ in range(4)]
        ft = [C([256]) for _ in range(4)]
        ptmp = C([512])
        habs = ptmp

        s.dma("sp", c4[:], s.d_c4[ps][:, :, :], "init20", [], ["c4"])
        s.dma("sp", s4[:], s.d_s4[ps][:, :, :], "init21", [], ["s4"])
        s.dma("sp", phi[:], s.d_phi[ps][:, :], "init22", [], ["phi"])
        s.dma("sp", zT[0:33, :], s.d_zT[ps][:, :], "init23", [], ["zT"])
        s.dma("sp", pw1[0:33, :], s.d_pe_w1[:, :], "init19", [], ["pw1"])
        s.dma("sp", pw2[0:64, :], s.d_pe_w2[:, :], "init18", [], ["pw2"])
        s.dma("sp", psm[0:64, :], s.d_pe_small[:, :], "init17", [], ["psm"])
        s.norm_mod(s.modA[:, s.ci, 0, 0, :], s.mod[:, s.ci, 0, 0, :], dst_bf16=s.hT)
        for li in range(2):
            s.ts("dve", pq[0:64, 2 * li:2 * li + 1], psm[0:64, 2:3], 1.0 / TWO_PI, None, ALU.mult, None, ["psm"], ["pq"])
            s.stt("dve", pq[0:64, 2 * li + 1:2 * li + 2], psm[0:64, li:li + 1], 1.0 / TWO_PI, psm[0:64, 2:3],
                  ALU.mult, ALU.mult, ["psm"], ["pq"])

        def mlp_layer(src, srcK, stok, w, wtok, dst, dtok, li):
            for c0 in range(0, L + 1, 512):
                n = min(512, L + 1 - c0)
                pb = s.B[6]
                s.mm(pb[0:64, 0:n], w[0:srcK, :], src[0:srcK, c0:c0 + n], True, True, [wtok, stok], ["B6"])
                s.ts("dve", mtmp[0:64, 0:n], pb[0:64, 0:n], pq[0:64, 2 * li:2 * li + 1], pq[0:64, 2 * li + 1:2 * li + 2],
                     ALU.mult, ALU.add, ["B6", "pq"], ["mtmp"])
                s.cp("dve", mtmpi[0:64, 0:n], mtmp[0:64, 0:n], ["mtmp"], ["mtmpi"])
                s.tt("dve", mtmp[0:64, 0:n], mtmp[0:64, 0:n], mtmpi[0:64, 0:n], ALU.subtract, ["mtmp", "mtmpi"], ["mtmp"])
                s.act(dst[0:64, c0:c0 + n], mtmp[0:64, 0:n], AF.Sin, ["mtmp"], [dtok], scale=TWO_PI)

        mlp_layer(zT, 33, "zT", pw1, "pw1", h1T, "h1T", 0)
        mlp_layer(h1T, 64, "h1T", pw2, "pw2", h2T, "h2T", 1)
        s.barrier()
        s.mark(ps + "hy_mlp")

        fb = lambda o, cb: s.spv("fbias", o * 8 + cb)
        nTT = NT // 128
        w3src = s.d_pe_w3.rearrange("j (o d c) -> j d o c", o=2, d=2)
        hsrc = s.d_hyin.rearrange("(k p) (q c) -> p k q c", p=128, q=3)

        for cb in range(8):
            csl = slice(cb * 128, (cb + 1) * 128)
            for q in range(3):
                s.dma("pool", wcb[0][:, :, q, :], hsrc[:, :, q, csl], "hyin%d" % q, [], ["wcb_q%d" % q])
            for d_ in range(2):
                s.dma("sp", w3cb[0:64, d_, :, :], w3src[:, d_, :, csl], "w3cb", [], ["w3cb"])
            s.dma("sp", decf[:], s.d_decf[ps][:, :, csl], "decf", [], ["decf"])
            s.dma("sp", decb[:], s.d_decb[ps][:, :, csl], "decb", [], ["decb"])
            pN = s.B[7]
            for kt in range(KT):
                alt = kt % 2
                pb = s.B[6] if alt == 0 else s.B[5]
                pbt_ = "B6" if alt == 0 else "B5"
                hd_ = hd if alt == 0 else cm[2][:, :].rearrange("p (d o c) -> p d o c", d=2, o=2)
                hdt_ = "hd" if alt == 0 else "cm2"
                hab_ = habs if alt == 0 else cm[3]
                habt_ = "ptmp" if alt == 0 else "cm3"
                s.mm(pb[:, 0:256], h2T[0:64, kt * 128:kt * 128 + 128], w3cb[0:64, 0, :, :], True, True,
                     ["h2T", "w3cb"], [pbt_])
                s.mm(pb[:, 256:512], h2T[0:64, kt * 128 + 1:kt * 128 + 129], w3cb[0:64, 1, :, :], True, True,
                     ["h2T", "w3cb"], [pbt_])
                for d_, dec in enumerate([decf, decb]):
                    dap = dec[:, kt, :]
                    din1 = AP(dap.tensor, dap.offset, [list(dap.ap[0]), [0, 2], [1, 128]])
                    s.tt("dve", hd_[:, d_, :, :], pb[:, d_ * 256:(d_ + 1) * 256].rearrange("p (o c) -> p o c", o=2), din1,
                         ALU.mult, [pbt_, "decf", "decb"], [hdt_])
                s.act(hab_[:], hd_[:].rearrange("p d o c -> p (d o c)"), AF.Abs, [hdt_], [habt_])
                s.mm(pN[:, 0:256], s.onesf[:], hab_[:, 0:256], kt == 0, False, ["onesf", habt_], ["B7"])
                s.mm(pN[:, 0:256], s.onesf[:], hab_[:, 256:512], False, False, ["onesf", habt_], ["B7"])
                s.tt("pool", apb[:, kt, :, :], hd_[:, 0, :, :], hd_[:, 1, :, :], ALU.add, [hdt_], ["apb"])
                s.tt("pool", amb[:, kt, :, :], hd_[:, 0, :, :], hd_[:, 1, :, :], ALU.subtract, [hdt_], ["amb"])
            pb = s.B[6]
            s.mm(pb[0:1, 0:256], h2T[0:64, 0:1], w3cb[0:64, 1, :, :], True, True, ["h2T", "w3cb"], ["B6"])
            s.act(arow[0:1, :], pb[0:1, 0:256], AF.Abs, ["B6"], ["arow"])
            s.mm(pN[:, 0:256], s.onesf[0:1, :], arow[0:1, :], False, True, ["onesf", "arow"], ["B7"])
            s.ts("dve", rn[:], pN[:, 0:256], EPS, float(L), ALU.add, ALU.mult, ["B7"], ["rn"])
            s.recip(rn[:], rn[:], ["rn"], ["rn"])
            s.mark(ps + "hy_cb%d_filt" % cb)
            slot = cb % 2
            wtok = "wcb0"
            c2_ = cm[2][:, :]
            zq_alt = AP(c2_.tensor, c2_.offset, [list(c2_.ap[0]), [1, NT]])
            for q, dst in enumerate([vT, x1T, x2T]):
                fi = q * 8 + cb
                zq_ = zq if q != 1 else zq_alt
                zt_ = ["zq"] if q != 1 else ["cm2", "cm3"]
                for tb in range(NTB):
                    tsl = slice(tb * 512, (tb + 1) * 512)
                    pb = s.bank(s.B[4:6])
                    pt = "B%d" % s.B.index(pb)
                    for k in range(8):
                        s.mm(pb[:, :], wcb[slot][:, k, q, :], s.hT[:, k, tsl], k == 0, k == 7, ["wcb_q%d" % q, "hT"], [pt])
                    s.act(zq_[:, tsl], pb[:, :], AF.Identity, [pt, "smallp"], zt_, bias=s.spv("hyinb", fi), scale=1.0)
                dtok = ["vT", "x1T", "x2T"][q]
                eng = "dve"
                s.act(dst[:], zq_[:, 0:NT], AF.Identity, zt_ + ["smallp"], [dtok], bias=s.spv("convb", fi), scale=s.spv("convw", 24 + fi))
                for (a, e) in s.seqs:
                    s.stt(eng, dst[:, a + 1:e], zq_[:, a:e - 1], s.spv("convw", fi), dst[:, a + 1:e], ALU.mult, ALU.add,
                          zt_ + [dtok, "smallp"], [dtok])
                    s.stt(eng, dst[:, a:e - 1], zq_[:, a + 1:e], s.spv("convw", 48 + fi), dst[:, a:e - 1], ALU.mult, ALU.add,
                          zt_ + [dtok, "smallp"], [dtok])
            s.cp("act", vb[:], vT[:], ["vT"], ["vb"])

            def sp_mm(m):
                pb = s.bank(s.B[4:6])
                pt = "B%d" % s.B.index(pb)
                for kt in range(KT):
                    s.mm(pb[:, 0:256], c4[:, kt, m * 128:(m + 1) * 128], apb[:, kt, :, :], kt == 0, kt == KT - 1,
                         ["c4", "apb"], [pt])
                for kt in range(KT):
                    s.mm(pb[:, 256:512], s4[:, kt, m * 128:(m + 1) * 128], amb[:, kt, :, :], kt == 0, kt == KT - 1,
                         ["s4", "amb"], [pt])
                return pb, pt

            def sp_bufs(m):
                if m % 2 == 0:
                    return [ft[i][:] for i in range(4)], ["ft0", "ft1", "ft2", "ft3"]
                return ([cm[0][:, 0:256], cm[0][:, 256:512], cm[1][:, 0:256], cm[1][:, 256:512]],
                        ["cm0", "cm0", "cm1", "cm1"])

            def sp_scale(m, pb, pt):
                f_, t_ = sp_bufs(m)
                s.tt("dve", f_[0], pb[:, 0:256], rn[:], ALU.mult, [pt, "rn"], [t_[0]])
                s.tt("dve", f_[1], pb[:, 256:512], rn[:], ALU.mult, [pt, "rn"], [t_[1]])

            def sp_rot(m):
                f_, t_ = sp_bufs(m)
                cph = phi[:, m:m + 1]
                sph = phi[:, KT + m:KT + m + 1]
                nsph = phi[:, 2 * KT + m:2 * KT + m + 1]
                s.act(f_[2], f_[0], AF.Identity, [t_[0], "phi"], [t_[2]], scale=cph)
                s.act(f_[3], f_[0], AF.Identity, [t_[0], "phi"], [t_[3]], scale=sph)
                s.stt("dve", Kr[:, m, :, :].rearrange("p o c -> p (o c)"), f_[1], nsph, f_[2], ALU.mult, ALU.add,
                      [t_[1], t_[2], "phi"], ["Kr"])
                s.stt("dve", Ki[:, m, :, :].rearrange("p o c -> p (o c)"), f_[1], cph, f_[3], ALU.mult, ALU.add,
                      [t_[1], t_[3], "phi"], ["Ki"])

            pbm = sp_mm(0)
            sp_scale(0, *pbm)
            for m in range(KT):
                if m + 1 < KT:
                    pbn = sp_mm(m + 1)
                    sp_scale(m + 1, *pbn)
                sp_rot(m)

            s.mark(ps + "hy_cb%d_in" % cb)
            for o in range(2):
                vsrc = vT if o == 0 else v2T
                vstok = "vT" if o == 0 else "v2T"
                gate = x1T if o == 0 else x2T
                gtok = "x1T" if o == 0 else "x2T"
                pbt = s.B[6]
                for tt_ in range(nTT):
                    bsel = pbt if tt_ < 4 else s.B[7]
                    btok = "B6" if tt_ < 4 else "B7"
                    s.mm(bsel[:, (tt_ % 4) * 128:(tt_ % 4 + 1) * 128], vb[:, tt_ * 128:(tt_ + 1) * 128], s.identb[:],
                         True, True, ["vb", "identb"], [btok])
                s.cp("act", vtok[:, 0:min(4, nTT), :].rearrange("p a c -> p (a c)"), pbt[:, 0:128 * min(4, nTT)],
                     ["B6"], ["vtok"])
                if nTT > 4:
                    s.cp("act", vtok[:, 4:8, :].rearrange("p a c -> p (a c)"), s.B[7][:, :], ["B7"], ["vtok"])
                for (a, e) in s.seqs:
                    tt0 = a // 128
                    HF = min(4, KT)
                    for half in range(KT // HF):
                        pr, pi = (s.B[0], s.B[1]) if half % 2 == 0 else (s.B[2], s.B[3])
                        tr, ti = ("B0", "B1") if half % 2 == 0 else ("B2", "B3")
                        for mi in range(HF):
                            m = half * HF + mi
                            for kt in range(KT):
                                s.mm(pr[:, mi * 128:(mi + 1) * 128], c4[:, kt, m * 128:(m + 1) * 128], vtok[:, tt0 + kt, :],
                                     kt == 0, kt == KT - 1, ["c4", "vtok"], [tr])
                            for kt in range(KT):
                                s.mm(pi[:, mi * 128:(mi + 1) * 128], s4[:, kt, m * 128:(m + 1) * 128], vtok[:, tt0 + kt, :],
                                     kt == 0, kt == KT - 1, ["s4", "vtok"], [ti])
                        W = HF * 128
                        msl = slice(half * HF, half * HF + HF)
                        vr = pr[:, 0:W].rearrange("p (m c) -> p m c", c=128)
                        vi = pi[:, 0:W].rearrange("p (m c) -> p m c", c=128)
                        c0 = cm[0][:, 0:W].rearrange("p (m c) -> p m c", c=128)
                        c1 = cm[1][:, 0:W].rearrange("p (m c) -> p m c", c=128)
                        c2 = cm[2][:, 0:W].rearrange("p (m c) -> p m c", c=128)
                        c3 = cm[3][:, 0:W].rearrange("p (m c) -> p m c", c=128)
                        s.tt("dve", c0, vr, Kr[:, msl, o, :], ALU.mult, [tr, "Kr"], ["cm0"])
                        s.tt("dve", c1, vi, Ki[:, msl, o, :], ALU.mult, [ti, "Ki"], ["cm1"])
                        s.tt("pool", Yb[:, msl, :], c0, c1, ALU.subtract, ["cm0", "cm1"], ["Yb"])
                        s.tt("dve", c2, vr, Ki[:, msl, o, :], ALU.mult, [tr, "Ki"], ["cm2"])
                        s.tt("dve", c3, vi, Kr[:, msl, o, :], ALU.mult, [ti, "Kr"], ["cm3"])
                        s.tt("pool", Yb[:, KT + half * HF:KT + half * HF + HF, :], c2, c3, ALU.add, ["cm2", "cm3"], ["Yb"])
                    NB = min(512, L)
                    for tblk in range(L // NB):
                        tsl = slice(a + tblk * NB, a + (tblk + 1) * NB)
                        fsl = slice(tblk * NB, (tblk + 1) * NB)
                        pb = s.bank(s.B[4:6])
                        pt = "B%d" % s.B.index(pb)
                        for kf in range(KT):
                            s.mm(pb[:, 0:NB], Yb[:, kf, :], c4[:, kf, fsl], kf == 0, False, ["Yb", "c4"], [pt])
                        for kf in range(KT):
                            s.mm(pb[:, 0:NB], Yb[:, KT + kf, :], s4[:, kf, fsl], False, kf == KT - 1, ["Yb", "s4"], [pt])
                        s.stt("dve", ptmp[:, 0:NB], vsrc[:, tsl], fb(o, cb), pb[:, 0:NB], ALU.mult, ALU.add,
                              [vstok, pt, "smallp"], ["ptmp"])
                        if o == 0:
                            s.tt("pool", v2T[:, tsl], ptmp[:, 0:NB], gate[:, tsl], ALU.mult, ["ptmp", gtok], ["v2T"])
                        else:
                            s.tt("pool", hyo[:, cb, tsl], ptmp[:, 0:NB], gate[:, tsl], ALU.mult, ["ptmp", gtok], ["hyo"])
                if o == 0:
                    s.cp("act", vb[:], v2T[:], ["v2T"], ["vb"])
                s.mark(ps + "hy_cb%d_o%d" % (cb, o))

        g1 = s.mod[:, s.ci, 0, 2, :]
        osrc = s.d_hyout.rearrange("(k p) n -> p k n", p=128)
        wo_alt = cm[0][:, :].bitcast(BF16)[:, 0:1024].rearrange("p (k c) -> p k c", k=8)
        wos = [(wo[0], "wo0", "hyout0"), (wo_alt, "cm0", "hyout1")]
        s.dma("pool", wos[0][0][:], osrc[:, :, 0:128], wos[0][2], [], [wos[0][1]])
        for m in range(8):
            slot = m % 2
            tok = wos[slot][1]
            if m + 1 < 8:
                nx = wos[(m + 1) % 2]
                s.dma("pool", nx[0][:], osrc[:, :, (m + 1) * 128:(m + 2) * 128], nx[2], [], [nx[1]])
            for tb in range(NTB):
                tsl = slice(tb * 512, (tb + 1) * 512)
                pb = s.bank(s.B[0:6])
                pt = "B%d" % s.B.index(pb)
                for k in range(8):
                    s.mm(pb[:, :], wos[slot][0][:, k, :], hyo[:, k, tsl], k == 0, k == 7, [tok, "hyo"], [pt])
                s.stt("dve", s.xT[:, m, tsl], pb[:, :], g1[:, m:m + 1], s.xT[:, m, tsl], ALU.mult, ALU.add,
                      [pt, "mod", "xT"], ["xT"])
            s.ts("dve", s.xT[:, m, 0:NT], s.xT[:, m, 0:NT], s.gb[:, s.ci, m:m + 1], None, ALU.add, None, ["xT", "gb"], ["xT"])

    def s5_prep(self):
        s = self
        s.barrier()
        off = [0]

        def C(shape, dt=F32, parts=128):
            esz = 4 if dt in (F32, I32) else 2
            n = 1
            for d in shape:
                n *= d
            v = s.carve(off[0], shape, dt, parts)
            off[0] += (n * esz + 3) // 4 * 4
            return v

        s5a = C([3, 128])
        s5e = C([2, 2, 8])
        cmpA = C([19, 128])
        Aout = C([12, 2, 2, 64])
        tb_ = C([15, 1024])
        s.dma("sp", s5a[0:64, :, :], s.d_s5a[:, :, :], "init16", [], ["s5a"])
        s.dma("sp", s5e[0:64, :, :, :].rearrange("p w d j -> p (w d j)"), s.d_s5e[:, :], "init15", [], ["s5e"])

        def cA(i):
            return cmpA[0:64, i, :]
        are, aim, ldt = s5a[0:64, 0, :], s5a[0:64, 1, :], s5a[0:64, 2, :]
        dt_, er, ph, mag, y_, yi_, sn, cs, lr, li = [cA(i) for i in range(10)]
        nr, den, cor, coi, Rr, Ri, t0_, t1_ = [cA(i) for i in range(10, 18)]
        yiI = cmpA[0:64, 18, :].bitcast(I32)

        def sincos(phase, scale, sn_o, cs_o, y, yI, toks_in):
            for (o, add) in ((sn_o, 0.0), (cs_o, 0.25)):
                s.ts("dve", y, phase, scale / TWO_PI, add, ALU.mult, ALU.add, toks_in, ["sc_y"])
                s.cp("dve", yI, y, ["sc_y"], ["sc_yi"])
                s.tt("dve", y, y, yI, ALU.subtract, ["sc_y", "sc_yi"], ["sc_y"])
                s.act(o, y, AF.Sin, ["sc_y"], ["sc_o"], scale=TWO_PI)

        s.act(dt_, ldt, AF.Exp, ["s5a"], ["cmp"])
        s.tt("dve", er, dt_, are, ALU.mult, ["cmp", "s5a"], ["cmp"])
        s.tt("dve", ph, dt_, aim, ALU.mult, ["cmp", "s5a"], ["cmp"])
        s.act(mag, er, AF.Exp, ["cmp"], ["cmp"])
        sincos(ph, 1.0, sn, cs, y_, yiI, ["cmp"])
        s.tt("dve", lr, mag, cs, ALU.mult, ["cmp", "sc_o"], ["cmp"])
        s.tt("dve", li, mag, sn, ALU.mult, ["cmp", "sc_o"], ["cmp"])
        s.ts("dve", nr, lr, -1.0, None, ALU.add, None, ["cmp"], ["cmp"])
        s.tt("dve", t0_, are, are, ALU.mult, ["s5a"], ["cmp"])
        s.tt("dve", den, aim, aim, ALU.mult, ["s5a"], ["cmp"])
        s.tt("dve", den, den, t0_, ALU.add, ["cmp"], ["cmp"])
        s.recip(den, den, ["cmp"], ["cmp"])
        s.tt("dve", t0_, nr, are, ALU.mult, ["cmp", "s5a"], ["cmp"])
        s.tt("dve", t1_, li, aim, ALU.mult, ["cmp", "s5a"], ["cmp"])
        s.tt("dve", t0_, t0_, t1_, ALU.add, ["cmp"], ["cmp"])
        s.tt("dve", cor, t0_, den, ALU.mult, ["cmp"], ["cmp"])
        s.tt("dve", t0_, li, are, ALU.mult, ["cmp", "s5a"], ["cmp"])
        s.tt("dve", t1_, nr, aim, ALU.mult, ["cmp", "s5a"], ["cmp"])
        s.tt("dve", t0_, t0_, t1_, ALU.subtract, ["cmp"], ["cmp"])
        s.tt("dve", coi, t0_, den, ALU.mult, ["cmp"], ["cmp"])
        s.act(mag, er, AF.Exp, ["cmp"], ["cmp"], scale=8.0)
        sincos(ph, 8.0, sn, cs, y_, yiI, ["cmp"])
        s.tt("dve", Rr, mag, cs, ALU.mult, ["cmp", "sc_o"], ["cmp"])
        s.tt("dve", Ri, mag, sn, ALU.mult, ["cmp", "sc_o"], ["cmp"])
        r8r, r8i = cA(8), cA(9)
        s.cp("dve", r8r, Rr, ["cmp"], ["cmp"])
        s.cp("dve", r8i, Ri, ["cmp"], ["cmp"])
        for _sq in range(3):
            s.tt("dve", t0_, r8r, r8r, ALU.mult, ["cmp"], ["cmp"])
            s.tt("dve", t1_, r8i, r8i, ALU.mult, ["cmp"], ["cmp"])
            s.tt("dve", y_, r8r, r8i, ALU.mult, ["cmp"], ["cmp"])
            s.tt("dve", r8r, t0_, t1_, ALU.subtract, ["cmp"], ["cmp"])
            s.ts("dve", r8i, y_, 2.0, None, ALU.mult, None, ["cmp"], ["cmp"])
        def emitA(ai, vr, vi):
            vrV = vr.rearrange("p (d g) -> p d g", d=2)
            viV = vi.rearrange("p (d g) -> p d g", d=2)
            for ri in range(2):
                s.cp("dve", Aout[0:64, 2 * ai, :, ri, :], vrV, ["cmp"], ["Aout"])
            s.ts("dve", Aout[0:64, 2 * ai + 1, :, 0, :], viV, -1.0, None, ALU.mult, None, ["cmp"], ["Aout"])
            s.cp("dve", Aout[0:64, 2 * ai + 1, :, 1, :], viV, ["cmp"], ["Aout"])

        emitA(0, Rr, Ri)
        emitA(1, r8r, r8i)
        for lv in range(1, 5):
            s.tt("dve", t0_, r8r, r8r, ALU.mult, ["cmp"], ["cmp"])
            s.tt("dve", t1_, r8i, r8i, ALU.mult, ["cmp"], ["cmp"])
            s.tt("dve", y_, r8r, r8i, ALU.mult, ["cmp"], ["cmp"])
            s.tt("dve", r8r, t0_, t1_, ALU.subtract, ["cmp"], ["cmp"])
            s.ts("dve", r8i, y_, 2.0, None, ALU.mult, None, ["cmp"], ["cmp"])
            emitA(1 + lv, r8r, r8i)
        s.dma("sp", s.d_s5A[:, :, :], Aout[0:64, :, :, :, :].rearrange("p a d r g -> p a (d r g)"), "s5A", ["Aout"], ["d_s5A"])

        erV = er.rearrange("p (d g) -> p d g", d=2)
        phV = ph.rearrange("p (d g) -> p d g", d=2)
        corV = cor.rearrange("p (d g) -> p d g", d=2)
        coiV = coi.rearrange("p (d g) -> p d g", d=2)

        def bc_last(ap3, n):
            a = ap3.ap
            return AP(ap3.tensor, ap3.offset, [list(a[0]), list(a[1]), list(a[2]), [0, n]])

        def T(i):
            return tb_[0:64, i, :].rearrange("p (d g j) -> p d g j", d=2, g=64)

        outs = {}
        for w in range(2):
            Ev = s5e[0:64, w, :, :]
            Eb = AP(Ev.tensor, Ev.offset, [list(Ev.ap[0]), list(Ev.ap[1]), [0, 64], list(Ev.ap[2])])
            earg, mg, yy, pr, pi_, sne, cse = T(0), T(1), T(2), T(3 + 4 * w), T(4 + 4 * w), T(11), T(12)
            yyI = tb_[0:64, 13, :].bitcast(I32).rearrange("p (d g j) -> p d g j", d=2, g=64)
            s.tt("dve", earg, Eb, bc_last(erV, 8), ALU.mult, ["s5e", "cmp"], ["tg"])
            s.act(mg, earg, AF.Exp, ["tg"], ["tg"])
            s.tt("dve", earg, Eb, bc_last(phV, 8), ALU.mult, ["s5e", "cmp"], ["tg"])
            for (o, add) in ((sne, 0.0), (cse, 0.25)):
                s.ts("dve", yy, earg, 1.0 / TWO_PI, add, ALU.mult, ALU.add, ["tg"], ["tg"])
                s.cp("dve", yyI, yy, ["tg"], ["tg"])
                s.tt("dve", yy, yy, yyI, ALU.subtract, ["tg"], ["tg"])
                s.act(o, yy, AF.Sin, ["tg"], ["tg"], scale=TWO_PI)
            s.tt("dve", pr, mg, cse, ALU.mult, ["tg"], ["tg"])
            s.tt("dve", pi_, mg, sne, ALU.mult, ["tg"], ["tg"])
            if w == 0:
                mr, mi = T(5), T(6)
                cr_b, ci_b = bc_last(corV, 8), bc_last(coiV, 8)
                s.tt("dve", T(0), pr, cr_b, ALU.mult, ["tg", "cmp"], ["tg"])
                s.tt("dve", T(1), pi_, ci_b, ALU.mult, ["tg", "cmp"], ["tg"])
                s.tt("dve", mr, T(0), T(1), ALU.subtract, ["tg"], ["tg"])
                s.tt("dve", T(0), pr, ci_b, ALU.mult, ["tg", "cmp"], ["tg"])
                s.tt("dve", T(1), pi_, cr_b, ALU.mult, ["tg", "cmp"], ["tg"])
                s.tt("dve", mi, T(0), T(1), ALU.add, ["tg"], ["tg"])
                outs[0], outs[1] = 5, 6
            else:
                s.ts("dve", T(10), pi_, -1.0, None, ALU.mult, None, ["tg"], ["tg"])
                outs[2], outs[3], outs[4] = 7, 8, 10
        for t_i in range(5):
            s.dma("sp", s.d_s5tab[:, t_i, :], tb_[0:64, outs[t_i], :], "scrw%d" % (t_i % 2), ["tg"], ["d_s5tab"])

    def s5_layer(self):
        s = self
        s.barrier()
        ps, NT, NTB = s.ps, s.NT, s.NTB
        CH = NT // 8
        nseq = len(s.seqs)
        Cl = CH // nseq
        s.norm_mod(s.modA[:, s.ci, 1, 0, :], s.mod[:, s.ci, 1, 0, :], dst_bf16=s.hT)
        uT = s.hT
        off = [0]

        def C(shape, dt=F32, parts=128):
            esz = 4 if dt in (F32, I32) else 2
            n = 1
            for d in shape:
                n *= d
            v = s.carve(off[0], shape, dt, parts)
            off[0] += (n * esz + 3) // 4 * 4
            return v

        XY = C([8, 1024], BF16)
        UY = C([8, NT], BF16)
        Ush = UY.rearrange("p k t -> p (k t)").rearrange("p (g c) -> p g c", c=CH)
        Aall = C([12, 2, 64])
        A1, A2 = Aall[:, 0, :, :], Aall[:, 1, :, :]
        AL = [(Aall[:, 2 + 2 * l, :, :], Aall[:, 3 + 2 * l, :, :]) for l in range(5)]
        tabs_t = C([5, 8, 8])
        mask = C([2, 128])
        NS = C([2, 2, 64]) if ps == "A" else None
        st0 = C([2, 64]) if ps == "B" else None
        off_glu = off[0]
        gen_here = not (ps == "B" and getattr(s, "s5w_cached", False))
        if gen_here:
            Bin = C([2, 8, 16])
            Cin = C([2, 8, 16])
            X1 = C([2, 8, 128], BF16)
            tA = [C([512]) for _ in range(2)]
            tB = [C([512]) for _ in range(2)]
        W4p = [C([2, 8, 128], BF16) for _ in range(2)]
        W2p = [C([2, 8, 128], BF16) for _ in range(2)]
        W1p = [C([2, 8, 128], BF16) for _ in range(2)]
        NB = Cl // 8
        QB = nseq * NB
        SBp = [C([8, 2, 8, QB]) for _ in range(2)]
        HBhp = [C([2, 8, nseq, Cl], BF16) for _ in range(2)]
        Pbp = [C([8, 2, 8, QB]) for _ in range(2)]
        HS = C([2, 8, nseq, NB + 1])
        Qb = C([2, 8, QB])
        rT = C([2, 8, QB])
        rU = C([2, 8, QB])
        rV = C([2, 8, QB])
        Ysh = C([8, CH], BF16)

        ib = s.identb
        asrc = s.d_s5A.rearrange("p a (d x) -> p a d x", d=2)
        for d_ in range(2):
            s.dma("sp", Aall[d_ * 64:(d_ + 1) * 64, :, :, :].rearrange("p a r g -> p a (r g)"), asrc[:, :, d_, :],
                  "s5A" if d_ == 0 else "s5A1", ["d_s5A"], ["A12"])
        s.dma("sp", mask[:], s.d_mask[:, :, :], "init14", [], ["mask"])
        if ps == "B":
            for d_ in range(2):
                s.dma("sp", st0[d_ * 64:(d_ + 1) * 64, :, :], s.d_s5st[:, d_, :, :], "s5st" if d_ == 0 else "s5st1", [], ["st0"])

        uv = [uT[:, k, 0:NT].rearrange("p (c j) -> p j c", j=8) for k in range(8)]
        for k in range(8):
            for jh in range(2):
                pb = s.bank(s.B[0:4])
                pt = "B%d" % s.B.index(pb)
                for jj in range(4):
                    j = jh * 4 + jj
                    s.mm(pb[0:CH, jj * 128:(jj + 1) * 128], uv[k][:, j, :], ib[:, :], True, True, ["hT", "identb"], [pt])
                eng = "dve" if (k * 2 + jh) % 2 == 0 else "act"
                xv = XY[0:CH, :, :]
                rs = xv.ap[1][0] * 8 // 8
                dst4 = AP(xv.tensor, xv.offset + k * 8 * 128 + jh * 4 * 16, [list(xv.ap[0]), [16, 4], [128, 8], [1, 16]])
                s.cp(eng, dst4, pb[0:CH, :].rearrange("p (a g h) -> p a g h", a=4, g=8), [pt], ["XY"])
        s.mark(ps + "s5_shufA")
        GPB = 512 // CH
        for g0 in range(0, 64, GPB):
            pb = s.bank(s.B[0:4])
            pt = "B%d" % s.B.index(pb)
            for gi in range(GPB):
                g = g0 + gi
                xg = XY[0:CH, :, :].rearrange("p j f -> p (j f)")[:, g * 128:(g + 1) * 128]
                s.mm(pb[:, gi * CH:(gi + 1) * CH], xg, ib[0:CH, 0:CH], True, True, ["XY", "identb"], [pt])
            eng = "dve" if (g0 // GPB) % 2 == 0 else "act"
            s.cp(eng, Ush[:, g0:g0 + GPB, :], pb[:, :].rearrange("p (a c) -> p a c", a=GPB), [pt], ["UY"])

        if ps == "A":
            s.memset("pool", HS[:, :, :, :, :], 0.0, ["HS"])

        def wgen(gb, par):
            gsl = slice(gb * 8, (gb + 1) * 8)
            W1, W2, W4 = W1p[par], W2p[par], W4p[par]
            t1_, t2_, t4_ = "W1_%d" % par, "W2_%d" % par, "W4_%d" % par
            if not gen_here:
                s.dma("sp", W1[:, :, :, :].rearrange("p d g q -> p (d g q)"), s.d_cw1[gb], "cw1_%d" % par, ["d_cw%d" % gb], [t1_])
                s.dma("sp", W2[:, :, :, :].rearrange("p r g q -> p (r g q)"), s.d_cw2[gb], "cw2_%d" % par, ["d_cw%d" % gb], [t2_])
                s.dma("sp", W4[:, :, :, :].rearrange("p r g q -> p (r g q)"), s.d_cw4[gb], "cw4_%d" % par, ["d_cw%d" % gb], [t4_])
                return
            tsrc = s.d_s5tab.rearrange("p t (d g j) -> p t d (g j)", d=2, g=64)
            for d_ in range(2):
                ph = slice(d_ * 64, (d_ + 1) * 64)
                for ri in range(2):
                    s.dma("sp", Bin[ph, ri, :, :], s.d_s5B[:, ri, d_, gsl, :], "s5b" if d_ == 0 else "s5b1", [], ["Bin"])
                    s.dma("sp", Cin[ph, ri, :, :], s.d_s5C[:, ri, d_, gsl, :], "s5c" if d_ == 0 else "s5c1", [], ["Cin"])
                s.dma("sp", tabs_t[ph, :, :, :].rearrange("p t g j -> p t (g j)"), tsrc[:, :, d_, gb * 64:(gb + 1) * 64],
                      "s5t%d" % d_, ["d_s5tab"], ["tabM", "tabL"])
            tabs = {"Mr": tabs_t[:, 0], "Mi": tabs_t[:, 1], "Lr": tabs_t[:, 2], "Li": tabs_t[:, 3], "nLi": tabs_t[:, 4]}

            def tab_b(tv, gs_):
                v = tv[:, gs_, :]
                return AP(v.tensor, v.offset, [list(v.ap[0]), list(v.ap[1]), list(v.ap[2]), [0, 16]])

            def in_b(tile_, ri, gs_):
                v = tile_[:, ri, gs_, :]
                return AP(v.tensor, v.offset, [list(v.ap[0]), list(v.ap[1]), [0, 8], list(v.ap[2])])

            def o4(tile_, ri, gs_):
                return tile_[:, ri, gs_, :].rearrange("p g (j h) -> p g j h", j=8)

            def t4(tl):
                return tl[:, :].rearrange("p (g j h) -> p g j h", g=4, j=8)

            for g2 in range(2):
                gs_ = slice(g2 * 4, g2 * 4 + 4)
                e = "dve"
                s.tt(e, t4(tA[0]), tab_b(tabs["Mr"], gs_), in_b(Bin, 0, gs_), ALU.mult, ["tabM", "Bin"], ["tA0"])
                s.tt(e, t4(tA[1]), tab_b(tabs["Mi"], gs_), in_b(Bin, 1, gs_), ALU.mult, ["tabM", "Bin"], ["tA1"])
                s.tt(e, o4(X1, 0, gs_), t4(tA[0]), t4(tA[1]), ALU.subtract, ["tA0", "tA1"], ["X1"])
                s.tt(e, t4(tA[0]), tab_b(tabs["Mr"], gs_), in_b(Bin, 1, gs_), ALU.mult, ["tabM", "Bin"], ["tA0"])
                s.tt(e, t4(tA[1]), tab_b(tabs["Mi"], gs_), in_b(Bin, 0, gs_), ALU.mult, ["tabM", "Bin"], ["tA1"])
                s.tt(e, o4(X1, 1, gs_), t4(tA[0]), t4(tA[1]), ALU.add, ["tA0", "tA1"], ["X1"])
                e = "pool"
                s.tt(e, t4(tB[0]), tab_b(tabs["Lr"], gs_), in_b(Cin, 0, gs_), ALU.mult, ["tabL", "Cin"], ["tB0"])
                s.tt(e, t4(tB[1]), tab_b(tabs["Li"], gs_), in_b(Cin, 1, gs_), ALU.mult, ["tabL", "Cin"], ["tB1"])
                s.tt(e, o4(W4, 0, gs_), t4(tB[0]), t4(tB[1]), ALU.subtract, ["tB0", "tB1"], [t4_])
                s.tt(e, t4(tB[0]), tab_b(tabs["nLi"], gs_), in_b(Cin, 0, gs_), ALU.mult, ["tabL", "Cin"], ["tB0"])
                s.tt(e, t4(tB[1]), tab_b(tabs["Lr"], gs_), in_b(Cin, 1, gs_), ALU.mult, ["tabL", "Cin"], ["tB1"])
                s.tt(e, o4(W4, 1, gs_), t4(tB[0]), t4(tB[1]), ALU.subtract, ["tB0", "tB1"], [t4_])
            for ri in range(2):
                for gh in range(2):
                    pb = s.bank(s.B[0:4])
                    pt = "B%d" % s.B.index(pb)
                    for gi in range(4):
                        g = gh * 4 + gi
                        s.mm(pb[:, gi * 128:(gi + 1) * 128], X1[:, ri, g, :], ib[:, :], True, True, ["X1", "identb"], [pt])
                    s.cp("act", W2[:, ri, gh * 4:gh * 4 + 4, :], pb[:, :].rearrange("p (g q) -> p g q", g=4), [pt], [t2_])
            for d_ in range(2):
                ph = slice(d_ * 64, (d_ + 1) * 64)
                for gh in range(2):
                    pb = s.bank(s.B[0:4])
                    pt = "B%d" % s.B.index(pb)
                    for gi in range(4):
                        g = gh * 4 + gi
                        s.mm(pb[:, gi * 128:(gi + 1) * 128], X1[ph, 0, g, :], W4[ph, 0, g, :], True, False, ["X1", t4_], [pt])
                        s.mm(pb[:, gi * 128:(gi + 1) * 128], X1[ph, 1, g, :], W4[ph, 1, g, :], False, True, ["X1", t4_], [pt])
                    mv = mask[:, d_, :]
                    mb = AP(mv.tensor, mv.offset, [list(mv.ap[0]), [0, 4], list(mv.ap[1])])
                    s.tt("dve", W1[:, d_, gh * 4:gh * 4 + 4, :], pb[:, :].rearrange("p (g q) -> p g q", g=4), mb, ALU.mult,
                         [pt, "mask"], [t1_])
            if ps == "A":
                s.dma("sp", s.d_cw1[gb], W1[:, :, :, :].rearrange("p d g q -> p (d g q)"), "cw1_%d" % par, [t1_], ["d_cw%d" % gb])
                s.dma("sp", s.d_cw2[gb], W2[:, :, :, :].rearrange("p r g q -> p (r g q)"), "cw2_%d" % par, [t2_], ["d_cw%d" % gb])
                s.dma("sp", s.d_cw4[gb], W4[:, :, :, :].rearrange("p r g q -> p (r g q)"), "cw4_%d" % par, [t4_], ["d_cw%d" % gb])

        def bc3(a_, n_):
            return AP(a_.tensor, a_.offset, [list(x) for x in a_.ap] + [[0, n_]])


        def ctx(gb):
            par = gb % 2
            return (slice(gb * 8, (gb + 1) * 8), par, W1p[par], W2p[par], W4p[par], "W1_%d" % par, "W2_%d" % par, "W4_%d" % par,
                    SBp[par], Pbp[par], HBhp[par], "SB%d" % par, "P%d" % par, "HBh%d" % par)

        def st_t2(gb):
            gsl, par, W1, W2, W4, tW1, tW2, tW4, SB, Pb, HBh, tSB, tP, tHB = ctx(gb)
            for ri in range(2):
                for g0 in range(0, 8, GPB):
                    pb = s.bank(s.B[0:4])
                    pt = "B%d" % s.B.index(pb)
                    for gi in range(GPB):
                        g = g0 + gi
                        s.mm(pb[:, gi * CH:(gi + 1) * CH], W2[:, ri, g, :], Ush[:, gb * 8 + g, :], True, True, [tW2, "UY"], [pt])
                    s.cp("act", SB[0:64, :, ri, g0:g0 + GPB, :], pb[0:64, :].rearrange("p (g x i) -> p i g x", g=GPB, i=8),
                         [pt], [tSB])
                    for q in range(nseq):
                        ov = SB[64:128, :, ri, g0:g0 + GPB, q * NB:(q + 1) * NB]
                        oa = ov.ap
                        orev = AP(ov.tensor, ov.offset + 7 * oa[1][0] + (NB - 1) * oa[3][0],
                                  [list(oa[0]), [-oa[1][0], 8], list(oa[2]), [-oa[3][0], NB]])
                        iv = pb[64:128, :].rearrange("p (g x i) -> p i g x", g=GPB, i=8)[:, :, :, q * NB:(q + 1) * NB]
                        s.cp("act", orev, iv, [pt], [tSB])

        def st_rec(gb):
            gsl, par, W1, W2, W4, tW1, tW2, tW4, SB, Pb, HBh, tSB, tP, tHB = ctx(gb)
            if ps == "B":
                s.cp("act", HS[:, :, :, 0, 0], st0[:, :, gsl], ["st0"], ["HS"])
            for ai_, asrc_ in enumerate((A1, A2)):
                pass

            def cmul(out, otok, in_all, itoks, a1, a2, nb_):
                a1b_ = bc3(a1[:, :, gsl], nb_)
                a2b_ = bc3(a2[:, :, gsl], nb_)
                uu, vv = rU[:, :, :, 0:nb_], rV[:, :, :, 0:nb_]
                ia = in_all.ap
                insw = AP(in_all.tensor, in_all.offset + ia[1][0], [list(ia[0]), [-ia[1][0], 2]] + [list(x) for x in ia[2:]])
                s.tt(e, uu, in_all, a1b_, ALU.mult, itoks + ["A12"], ["rU"])
                s.tt(e, vv, insw, a2b_, ALU.mult, itoks + ["A12"], ["rV"])
                s.tt(e, out, uu, vv, ALU.add, ["rU", "rV"], [otok])

            for k in range(8):
                if k == 0:
                    cmul(Pb[:, 0], tP, SB[:, 0], [tSB], A1, A2, QB)
                else:
                    s.tt(e, rT[:, :, :, :], Pb[:, k - 1], SB[:, k], ALU.add, [tP, tSB], ["rT"])
                    cmul(Pb[:, k], tP, rT[:, :, :, :], ["rT"], A1, A2, QB)
            P8 = Pb[:, 7].rearrange("p r g (q b) -> p r g q b", q=nseq)
            if NB <= 4:
                for st in range(NB):
                    hin = HS[:, :, :, :, st]
                    tq = rT[:, :, :, 0:nseq]
                    cmul(tq, "rT", hin, ["HS"], AL[0][0], AL[0][1], nseq)
                    s.tt(e, HS[:, :, :, :, st + 1], tq, P8[:, :, :, :, st], ALU.add, ["rT", tP], ["HS"])
            else:
                ncol = NB + 1
                for q in range(nseq):
                    s.cp(e, HS[:, :, :, q, 1:ncol], P8[:, :, :, q, :], [tP], ["HS"])
                lv = 0
                while (1 << lv) < ncol:
                    sh = 1 << lv
                    nn = ncol - sh
                    for q in range(nseq):
                        tq = rT[:, :, :, 0:nn]
                        cmul(tq, "rT", HS[:, :, :, q, 0:nn], ["HS"], AL[lv][0], AL[lv][1], nn)
                        s.tt(e, HS[:, :, :, q, sh:ncol], HS[:, :, :, q, sh:ncol], tq, ALU.add, ["rT", "HS"], ["HS"])
                    lv += 1
            for q in range(nseq):
                s.cp(e, Qb[:, :, :, q * NB:(q + 1) * NB], HS[:, :, :, q, 0:NB], ["HS"], ["Q"])
            s.cp(e, Pb[:, 7], Qb[:, :, :, :], ["Q", "HS"], [tP])
            for k in range(1, 8):
                qa = Qb[:, :, :, :]
                cmul(qa, "Q", qa, ["Q"], A1, A2, QB)
                s.tt(e, Pb[:, k - 1], qa, Pb[:, k - 1], ALU.add, ["Q", tP], [tP])
            for ri in range(2):
                for q in range(nseq):
                    dst = HBh[0:64, ri, :, q, :].rearrange("p g (b i) -> p g b i", i=8)
                    pv = Pb[0:64, :, ri, :, q * NB:(q + 1) * NB]
                    pa = pv.ap
                    s.cp("act", dst[:, :, :, 0], pv[:, 7, :, :], [tP], [tHB])
                    src = AP(pv.tensor, pv.offset, [list(pa[0]), list(pa[2]), list(pa[3]), [pa[1][0], 7]])
                    s.cp("act", dst[:, :, :, 1:8], src, [tP], [tHB])
                    hv = HBh[64:128, ri, :, q, :]
                    ha = hv.ap
                    pv = Pb[64:128, :, ri, :, q * NB:(q + 1) * NB]
                    pa = pv.ap
                    d0 = AP(hv.tensor, hv.offset + (Cl - 1) * ha[2][0], [list(ha[0]), list(ha[1]), [-8 * ha[2][0], NB]])
                    s.cp("act", d0, pv[:, 7, :, :], [tP], [tHB])
                    d1 = AP(hv.tensor, hv.offset + (Cl - 2) * ha[2][0], [list(ha[0]), list(ha[1]), [-8 * ha[2][0], NB], [-ha[2][0], 7]])
                    src = AP(pv.tensor, pv.offset, [list(pa[0]), list(pa[2]), list(pa[3]), [pa[1][0], 7]])
                    s.cp("act", d1, src, [tP], [tHB])
            if ps == "A":
                src = HS[:, :, :, :, NB]
                srcp = AP(src.tensor, src.offset, [list(src.ap[0]), list(src.ap[3]), list(src.ap[1]), list(src.ap[2])])
                s.cp("act", NS[:, :, :, gsl], srcp, ["HS"], ["NS"])

        def st_out(gb):
            gsl, par, W1, W2, W4, tW1, tW2, tW4, SB, Pb, HBh, tSB, tP, tHB = ctx(gb)
            s.mark(ps + "s5_b%d_t2" % gb)
            for g0 in range(0, 8, GPB):
                pb = s.bank(s.B[4:8])
                pt = "B%d" % s.B.index(pb)
                for gi in range(GPB):
                    g = g0 + gi
                    for q in range(nseq):
                        o_ = pb[:, gi * CH + q * Cl:gi * CH + (q + 1) * Cl]
                        us = Ush[:, gb * 8 + g, q * Cl:(q + 1) * Cl]
                        s.mm(o_, W1[:, 0, g, :], us, True, False, [tW1, "UY"], [pt])
                        s.mm(o_, W1[:, 1, g, :], us, False, False, [tW1, "UY"], [pt])
                        s.mm(o_, W4[:, 0, g, :], HBh[:, 0, g, q, :], False, False, [tW4, tHB], [pt])
                        s.mm(o_, W4[:, 1, g, :], HBh[:, 1, g, q, :], False, True, [tW4, tHB], [pt])
                s.cp("act", Ysh[:, g0:g0 + GPB, :], pb[:, :].rearrange("p (g c) -> p g c", g=GPB), [pt], ["Ysh"])
            s.mark(ps + "s5_b%d_t14" % gb)
            for gh in range(2):
                pb = s.bank(s.B[4:8])
                pt = "B%d" % s.B.index(pb)
                for gi in range(4):
                    g = gh * 4 + gi
                    s.mm(pb[0:CH, gi * 128:(gi + 1) * 128], Ysh[:, g, :], ib[:, :], True, True, ["Ysh", "identb"], [pt])
                G0 = gb * 8 + gh * 4
                dstv = XY[0:CH, :, G0 * 16:(G0 + 4) * 16]
                dst4 = AP(dstv.tensor, dstv.offset, [list(dstv.ap[0]), [16, 4], list(dstv.ap[1]), [1, 16]])
                s.cp("act", dst4, pb[0:CH, :].rearrange("p (g i h) -> p g i h", g=4, i=8), [pt], ["XY"])


        e = "dve"
        wgen(0, 0)
        st_t2(0)
        for gb in range(8):
            if gb + 1 < 8:
                wgen(gb + 1, (gb + 1) % 2)
                st_t2(gb + 1)
            st_rec(gb)
            st_out(gb)

        if ps == "A":
            s.s5w_cached = True
            for d_ in range(2):
                s.dma("sp", s.d_ns[:, :, d_, :, :].rearrange("p q r g -> p q (r g)"),
                      NS[d_ * 64:(d_ + 1) * 64, :, :, :].rearrange("p q r g -> p q (r g)"), "nsout" if d_ == 0 else "nsout1", ["NS"], [])

        s.mark(ps + "s5_batches")
        IPB = 512 // CH
        for k in range(8):
            for i0 in range(0, 8, IPB):
                pb = s.bank(s.B[0:4])
                pt = "B%d" % s.B.index(pb)
                for ii in range(IPB):
                    i = i0 + ii
                    s.mm(pb[:, ii * CH:(ii + 1) * CH], XY[0:CH, i, k * 128:(k + 1) * 128], ib[0:CH, 0:CH], True, True,
                         ["XY", "identb"], [pt])
                dv = UY[:, k, 0:NT].rearrange("p (c i) -> p i c", i=8)[:, i0:i0 + IPB, :]
                s.cp("dve" if (k % 2 == 0) else "act", dv, pb[:, :].rearrange("p (i c) -> p i c", i=IPB), [pt], ["UY"])

        if s.stop_after == ps + "s5y":
            for k in range(8):
                s.cp("dve", s.xT[:, k, 0:NT], UY[:, k, :], ["UY"], ["xT"])
            return
        s.barrier()
        off[0] = off_glu
        gt = [[C([NT]) for _ in range(3)] for _ in range(2)]
        wg = [C([8, 2, 128], BF16) for _ in range(2)]
        wgs = [C([8, 2, 128]) for _ in range(2)]
        gs = [C([512]) for _ in range(4)]
        g1 = s.mod[:, s.ci, 1, 2, :]
        gsrc = s.d_glu.rearrange("(k p) (h n) -> p k h n", p=128, h=2)

        def loadg(m):
            ss = m % 2
            for h_ in range(2):
                s.dma("sp", wgs[ss][:, :, h_, :], gsrc[:, :, h_, m * 128:(m + 1) * 128], "glus%dh%d" % (ss, h_), [],
                      ["glus%dh%d" % (ss, h_)])

        loadg(0)
        loadg(1)
        for k in range(8):
            b_ = k % 2
            t, x2, q = gt[b_][0][:, :], gt[b_][1][:, :], gt[b_][2][:, :]
            tk = ["gt%d_%d" % (b_, i) for i in range(3)]
            s.stt("dve", t, uT[:, k, 0:NT], s.spv("s5D", k), UY[:, k, :], ALU.mult, ALU.add, ["hT", "UY", "smallp"], [tk[0]])
            s.act(x2, t, AF.Square, [tk[0]], [tk[1]])
            s.ts("dve", x2, x2, 0.044715, 1.0, ALU.mult, ALU.add, [tk[1]], [tk[1]])
            s.tt("pool", q, x2, t, ALU.mult, [tk[1], tk[0]], [tk[2]])
            s.act(q, q, AF.Sigmoid, [tk[2]], [tk[2]], scale=2.0 * 0.7978845608028654)
            s.tt("dve", UY[:, k, :], q, t, ALU.mult, [tk[2], tk[0]], ["UY"])
        s.mark(ps + "s5_gelu")
        for m in range(8):
            slot = m % 2
            tok = "wg%d" % slot
            s.cp("act", wg[slot][:], wgs[slot][:], ["glus%dh0" % slot, "glus%dh1" % slot], [tok])
            if m + 2 < 8:
                loadg(m + 2)
            for tb in range(NTB):
                tsl = slice(tb * 512, (tb + 1) * 512)
                ba = s.bank(s.B[0:6])
                bb = s.bank(s.B[0:6])
                ta, tbk = "B%d" % s.B.index(ba), "B%d" % s.B.index(bb)
                for k in range(8):
                    s.mm(ba[:, :], wg[slot][:, k, 0, :], UY[:, k, tsl], k == 0, k == 7, [tok, "UY"], [ta])
                for k in range(8):
                    s.mm(bb[:, :], wg[slot][:, k, 1, :], UY[:, k, tsl], k == 0, k == 7, [tok, "UY"], [tbk])
                ix = (m * NTB + tb) % 2
                sg, tm = gs[ix][:, :], gs[2 + ix][:, :]
                s.act(sg, bb[:, :], AF.Sigmoid, [tbk, "smallp"], ["gsg%d" % ix], bias=s.spv("glub", 8 + m), scale=1.0)
                s.stt("dve", tm, ba[:, :], s.spv("glub", m), sg, ALU.add, ALU.mult, [ta, "gsg%d" % ix, "smallp"], ["gtm%d" % ix])
                s.stt("dve", s.xT[:, m, tsl], tm, g1[:, m:m + 1], s.xT[:, m, tsl], ALU.mult, ALU.add,
                      ["gtm%d" % ix, "mod", "xT"], ["xT"])


_CACHE = {}


def _feat_pk(v):
    v = np.asarray(v, np.float32)
    return np.ascontiguousarray(v.reshape(-1, 128).T)


def _host_inputs(inp):
    f32 = np.float32
    g = {}
    sp = np.zeros((128, NSP), f32)

    def put(name, arr):
        a = _feat_pk(arr)
        sp[:, SP_OFF[name]:SP_OFF[name] + a.shape[1]] = a

    put("n1g", inp["norm1_g"].reshape(-1))
    put("n2g", inp["norm2_g"].reshape(-1))
    put("fing", inp["final_g"].reshape(-1))
    put("adab", inp["ada_b"].reshape(-1))
    put("hyinb", inp["hy_in_b"].reshape(-1))
    put("convw", inp["hy_conv_w"].reshape(-1))
    put("convb", inp["hy_conv_b"].reshape(-1))
    put("fbias", inp["hy_fbias"].reshape(-1))
    put("hyoutb", inp["hy_out_b"].reshape(-1))
    put("s5D", inp["s5_D"].reshape(-1))
    put("glub", inp["s5_glu_b"].reshape(-1))
    g["smallp"] = sp
    g["ident"] = np.eye(128, dtype=f32)
    g["posT"] = _pos_embed_T()
    g["ada_w"] = np.ascontiguousarray(inp["ada_w"], f32)
    g["ffn_w13"] = np.ascontiguousarray(inp["ffn_w13"], f32)
    g["ffn_w2"] = np.ascontiguousarray(inp["ffn_w2"], f32)
    g["hy_in_w"] = np.ascontiguousarray(inp["hy_in_w"][0], f32)
    g["hy_out_w"] = np.ascontiguousarray(inp["hy_out_w"][0], f32)
    g["s5_glu_w"] = np.ascontiguousarray(inp["s5_glu_w"][0], f32)
    g["pe_w1"] = np.ascontiguousarray(inp["hy_pe_w1"][0], f32)
    g["pe_w2"] = np.ascontiguousarray(inp["hy_pe_w2"][0], f32)
    g["pe_w3"] = np.ascontiguousarray(inp["hy_pe_w3"][0], f32)
    g["pe_small"] = np.ascontiguousarray(
        np.stack([inp["hy_pe_b1"][0], inp["hy_pe_b2"][0], inp["hy_freq"][0]], axis=1), f32)
    for nm, L in (("A", 256), ("B", 1024)):
        c4, s4, phi = _dft_consts(L)
        zT, decf, decb = _hyena_consts(L)
        g["c4_" + nm] = c4.astype(ml_dtypes.bfloat16)
        g["s4_" + nm] = s4.astype(ml_dtypes.bfloat16)
        g["phi_" + nm] = phi
        g["zT_" + nm] = zT
        g["decf_" + nm] = decf
        g["decb_" + nm] = decb
    are = np.transpose(inp["s5_A_re"][0], (2, 0, 1)).reshape(64, 128)
    aim = np.transpose(inp["s5_A_im"][0], (2, 0, 1)).reshape(64, 128)
    ldt = np.broadcast_to(inp["s5_log_dt"][0].reshape(1, 128), (64, 128))
    g["s5_a"] = np.ascontiguousarray(np.stack([are, aim, ldt], axis=1), f32)
    ea = np.zeros((2, 64, 8), f32)
    ef = np.zeros((2, 64, 8), f32)
    for j in range(8):
        ea[0, :, j] = -1 - j
        ea[1, :, j] = j - 8
        ef[0, :, j] = j + 1
        ef[1, :, j] = 8 - j
    e = np.stack([ea[:, 0, :], ef[:, 0, :]], axis=0).reshape(-1)
    g["s5_e"] = np.ascontiguousarray(np.broadcast_to(e[None], (64, 32)), f32)
    Bt = np.stack([np.transpose(inp["s5_B_re"][0], (2, 0, 1, 3)), np.transpose(inp["s5_B_im"][0], (2, 0, 1, 3))], axis=1)
    Ct = np.stack([np.transpose(inp["s5_C_re"][0], (3, 0, 1, 2)), np.transpose(inp["s5_C_im"][0], (3, 0, 1, 2))], axis=1)
    g["s5_Bt"] = np.ascontiguousarray(Bt, f32)
    g["s5_Ct"] = np.ascontiguousarray(Ct, f32)
    mask = np.zeros((128, 2, 128), f32)
    for j in range(8):
        for i in range(8):
            if j <= i:
                mask[j * 16:(j + 1) * 16, 0, i * 16:(i + 1) * 16] = 1.0
            if j >= i:
                mask[j * 16:(j + 1) * 16, 1, i * 16:(i + 1) * 16] = 1.0
    g["w1mask"] = mask
    return g


def kernel(**inp):
    stop_after = os.environ.get("KSTOP") or None
    inp = {k: np.asarray(v) for k, v in inp.items()}
    key = stop_after
    if key not in _CACHE:
        _CACHE[key] = Builder(stop_after).build()
    nc = _CACHE[key]
    g = _host_inputs(inp)
    xp = inp["x_prompt"].astype(np.float32)
    xs = inp["x_sample"].astype(np.float32)
    in_maps = []
    for c in range(8):
        m = dict(g)
        m["xT_A"] = np.ascontiguousarray(xp[2 * c:2 * c + 2].reshape(512, D).T)
        m["xT_B"] = np.ascontiguousarray(xs[c].T)
        cond = np.stack([inp["c_ctx"].astype(np.float32), inp["c"][c].astype(np.float32)], axis=1)
        m["condT"] = np.ascontiguousarray(cond.reshape(8, 128, 2).transpose(1, 0, 2))
        m["s5_st0"] = np.ascontiguousarray(np.transpose(inp["state_s5"][c, 0], (3, 0, 1, 2)), np.float32)
        in_maps.append(m)
    res = run_bass_kernel_spmd(nc, in_maps, core_ids=list(range(8)))
    R = res.results
    if stop_after is not None:
        return [np.asarray(r["dbg"]) for r in R]
    y_prompt = np.zeros((16, 256, D), np.float32)
    y_sample = np.zeros((8, 1024, D), np.float32)
    new_state = np.zeros((16, 1, 2, 2, 64, 64), np.float32)
    for c in range(8):
        y_prompt[2 * c:2 * c + 2] = np.asarray(R[c]["yT_A"]).T.reshape(2, 256, D)
        y_sample[c] = np.asarray(R[c]["yT_B"]).T
        ns = np.asarray(R[c]["ns_out"])
        new_state[2 * c:2 * c + 2, 0] = np.transpose(ns, (1, 2, 3, 4, 0))
    return (y_prompt, y_sample, new_state)
```
